# Optimizing a Trainium2 kernel written in Bass

```python
import jax, jax.numpy as jnp
from jax import lax
import numpy as np

D_MODEL = 2048
BATCH = 4
SEQ = 2048
DEPTH = 2

CHUNK = 64
RET_HEADS = 8
RET_HEAD_DIM = D_MODEL // 16
RET_WIDTH = RET_HEADS * RET_HEAD_DIM
LRU_WIDTH = D_MODEL // 2
LRU_GROUPS = 16
LRU_GROUP_DIM = LRU_WIDTH // LRU_GROUPS
CONV_W = 4
LRU_C = 8.0
MIX_WIDTH = RET_WIDTH + LRU_WIDTH
IN_WIDTH = 4 * RET_WIDTH + 2 * LRU_WIDTH
D_FF = ((8 * D_MODEL // 3 + 255) // 256) * 256
ROPE_BASE = 10000.0
EPS = 1e-6

kernel_name = "hybrid_retention_rglru_swiglu"


def _rmsnorm(x, g):
    xf = x.astype(jnp.float32)
    y = xf * lax.rsqrt(jnp.mean(xf * xf, axis=-1, keepdims=True) + EPS)
    return (y * g.astype(jnp.float32)).astype(x.dtype)


def _rotary(t, pos):
    dk = t.shape[-1]
    inv = 1.0 / (ROPE_BASE ** (jnp.arange(0, dk, 2, dtype=jnp.float32) / dk))
    ang = pos[:, None] * inv[None, :]
    cos = jnp.cos(ang)[None, :, None, :]
    sin = jnp.sin(ang)[None, :, None, :]
    t1, t2 = t[..., : dk // 2], t[..., dk // 2:]
    return jnp.concatenate([t1 * cos - t2 * sin, t1 * sin + t2 * cos], axis=-1)


def _retention(q, k, v, g, gn_g):
    B, T, _ = q.shape
    H, dk, C = RET_HEADS, RET_HEAD_DIM, CHUNK
    nc = T // C
    pos = jnp.arange(T, dtype=jnp.float32)
    q = _rotary(q.reshape(B, T, H, dk), pos)
    k = _rotary(k.reshape(B, T, H, dk), pos) * (dk ** -0.5)
    v = v.reshape(B, T, H, dk)
    qc = q.reshape(B, nc, C, H, dk)
    kc = k.reshape(B, nc, C, H, dk)
    vc = v.reshape(B, nc, C, H, dk)

    log_g = jnp.log1p(-jnp.exp2(-5.0 - jnp.arange(H, dtype=jnp.float32)))
    idx = jnp.arange(C, dtype=jnp.float32)
    dist = jnp.abs(idx[:, None] - idx[None, :])
    d_intra = jnp.exp(log_g[:, None, None] * dist)

    scores = jnp.einsum('bnahd,bnchd->bnhac', qc, kc) * d_intra
    o_intra = jnp.einsum('bnhac,bnche->bnahe', scores, vc)

    k_dec = jnp.exp(log_g[:, None] * (C - 1.0 - idx)[None, :])
    u = jnp.einsum('bnchd,hc,bnche->nbhde', kc, k_dec, vc)
    chunk_dec = jnp.exp(log_g * C)[None, :, None, None]

    def step(s, u_j):
        return chunk_dec * s + u_j, s

    _, s_in = lax.scan(step, jnp.zeros(u.shape[1:], u.dtype), u)
    q_dec = jnp.exp(log_g[:, None] * (idx + 1.0)[None, :])
    o_cross = jnp.einsum('bnahd,ha,nbhde->bnahe', qc, q_dec, s_in)

    o = (o_intra + o_cross).reshape(B, T, H, dk)
    mu = jnp.mean(o, axis=-1, keepdims=True)
    var = jnp.mean(jnp.square(o - mu), axis=-1, keepdims=True)
    on = ((o - mu) * lax.rsqrt(var + EPS)).reshape(B, T, RET_WIDTH) * gn_g
    return on * jax.nn.silu(g)


def _rg_lru_branch(xb, yb, conv_w, conv_b, wa, ba, wx, bx, lam, norm_g):
    B, T, W = xb.shape
    xp = jnp.pad(xb, ((0, 0), (CONV_W - 1, 0), (0, 0)))
    xc = conv_b + sum(xp[:, j:j + T] * conv_w[j] for j in range(CONV_W))
    xg = xc.reshape(B, T, LRU_GROUPS, LRU_GROUP_DIM)
    r = jax.nn.sigmoid(jnp.einsum('btgi,gij->btgj', xg, wa).reshape(B, T, W) + ba)
    i = jax.nn.sigmoid(jnp.einsum('btgi,gij->btgj', xg, wx).reshape(B, T, W) + bx)
    log_a = -LRU_C * r * jax.nn.softplus(-lam)
    a = jnp.exp(log_a)
    b = jnp.sqrt(-jnp.expm1(2.0 * log_a)) * (i * xc)

    def comb(left, right):
        a1, b1 = left
        a2, b2 = right
        return a1 * a2, a2 * b1 + b2

    _, h = lax.associative_scan(comb, (a, b), axis=1)
    y = h * jax.nn.gelu(yb)
    y = y * lax.rsqrt(jnp.mean(y * y, axis=-1, keepdims=True) + EPS)
    return y * norm_g


def setup_inputs(seed: int = 0) -> dict:
    key = jax.random.key(seed)
    ks = jax.random.split(key, 20)
    f32 = jnp.float32

    def nrm(k, shape, scale):
        return jax.random.normal(k, shape, f32) * scale

    def gain(k, shape):
        return 1.0 + 0.02 * jax.random.normal(k, shape, f32)

    u = jax.random.uniform(ks[13], (DEPTH, LRU_WIDTH), f32, 0.9, 0.999)
    a0 = u ** (1.0 / LRU_C)
    lam = jnp.log(a0) - jnp.log1p(-a0)
    return {
        "x": nrm(ks[0], (BATCH, SEQ, D_MODEL), 1.0),
        "norm1_g": gain(ks[1], (DEPTH, D_MODEL)),
        "w_in": nrm(ks[2], (DEPTH, D_MODEL, IN_WIDTH), D_MODEL ** -0.5),
        "ret_gn_g": gain(ks[3], (DEPTH, RET_WIDTH)),
        "lru_conv_w": nrm(ks[4], (DEPTH, CONV_W, LRU_WIDTH), CONV_W ** -0.5),
        "lru_conv_b": nrm(ks[5], (DEPTH, LRU_WIDTH), 0.01),
        "lru_wa": nrm(ks[6], (DEPTH, LRU_GROUPS, LRU_GROUP_DIM, LRU_GROUP_DIM), LRU_GROUP_DIM ** -0.5),
        "lru_ba": nrm(ks[7], (DEPTH, LRU_WIDTH), 0.01),
        "lru_wx": nrm(ks[8], (DEPTH, LRU_GROUPS, LRU_GROUP_DIM, LRU_GROUP_DIM), LRU_GROUP_DIM ** -0.5),
        "lru_bx": nrm(ks[9], (DEPTH, LRU_WIDTH), 0.01),
        "lru_lambda": lam,
        "lru_norm_g": gain(ks[10], (DEPTH, LRU_WIDTH)),
        "w_out": nrm(ks[11], (DEPTH, MIX_WIDTH, D_MODEL), MIX_WIDTH ** -0.5),
        "norm2_g": gain(ks[12], (DEPTH, D_MODEL)),
        "ffn_w_gate": nrm(ks[14], (DEPTH, D_MODEL, D_FF), D_MODEL ** -0.5),
        "ffn_w_up": nrm(ks[15], (DEPTH, D_MODEL, D_FF), D_MODEL ** -0.5),
        "ffn_w_down": nrm(ks[16], (DEPTH, D_FF, D_MODEL), D_FF ** -0.5),
        "final_g": gain(ks[17], (D_MODEL,)),
    }


def reference(x, norm1_g, w_in, ret_gn_g, lru_conv_w, lru_conv_b, lru_wa, lru_ba,
              lru_wx, lru_bx, lru_lambda, lru_norm_g, w_out, norm2_g,
              ffn_w_gate, ffn_w_up, ffn_w_down, final_g):
    f32 = jnp.float32
    R, L = RET_WIDTH, LRU_WIDTH
    for l in range(DEPTH):
        h = _rmsnorm(x, norm1_g[l])
        p = (h @ w_in[l]).astype(f32)
        q, k, v, g, xb, yb = jnp.split(p, [R, 2 * R, 3 * R, 4 * R, 4 * R + L], axis=-1)
        o_ret = _retention(q, k, v, g, ret_gn_g[l].astype(f32))
        o_lru = _rg_lru_branch(xb, yb, lru_conv_w[l].astype(f32), lru_conv_b[l].astype(f32),
                               lru_wa[l].astype(f32), lru_ba[l].astype(f32),
                               lru_wx[l].astype(f32), lru_bx[l].astype(f32),
                               lru_lambda[l].astype(f32), lru_norm_g[l].astype(f32))
        mix = jnp.concatenate([o_ret, o_lru], axis=-1).astype(x.dtype)
        x = x + mix @ w_out[l]
        h = _rmsnorm(x, norm2_g[l])
        x = x + (jax.nn.silu(h @ ffn_w_gate[l]) * (h @ ffn_w_up[l])) @ ffn_w_down[l]
    return _rmsnorm(x, final_g)
```

```python
import os
import numpy as np
import concourse.bass as bass
import concourse.mybir as mybir
from concourse.bass_utils import run_bass_kernel_spmd

F32 = mybir.dt.float32
BF16 = mybir.dt.bfloat16
AF = mybir.ActivationFunctionType
ALU = mybir.AluOpType

D = 2048
KC = 16
T = 1024
NT = 2
DEPTH = 2
NH = 8
DK = 128
CH = 64
DFF = 5632
FH = 11
FQ = 4
NB = 4
EPS = 1e-6
NCORES = 8

PV_N1G, PV_N2G, PV_GNG, PV_CW, PV_CB, PV_BA, PV_BX, PV_LAM, PV_LNG = 0, 16, 32, 40, 72, 80, 88, 96, 104
PV_L = 112
PV_FINAL = PV_L * DEPTH
PV_TOT = PV_FINAL + 16


class Sched:
    def __init__(self):
        self.ops = {e: [] for e in ("pe", "act", "dve", "pool", "sp")}
        self.cnt = {}
        self.step = {}
        self.seen = {e: {} for e in self.ops}
        self.lastw = {}
        self.readers = {}
        self.small_ev = set()

    def _sem(self, name, step):
        if name not in self.cnt:
            self.cnt[name] = 0
            self.step[name] = step

    def _deps(self, eng, reads, writes, small=False):
        waits = {}

        def need(ev, war=False):
            s, c = ev
            if s == eng and eng == "pe":
                return
            if c > self.seen[eng].get(s, 0):
                waits[s] = max(waits.get(s, 0), c)

        for k in reads:
            if k in self.lastw:
                need(self.lastw[k])
        for k in writes:
            if k in self.lastw:
                need(self.lastw[k])
            for r in self.readers.get(k, ()):
                need(r, war=True)
        for s, c in waits.items():
            self.seen[eng][s] = c
        return waits

    def _commit(self, ev, reads, writes):
        for k in writes:
            self.lastw[k] = ev
            self.readers[k] = []
        for k in reads:
            self.readers.setdefault(k, []).append(ev)

    def op(self, eng, fn, reads=(), writes=(), small=False):
        self._sem(eng, 1)
        waits = self._deps(eng, reads, writes, small)
        self.cnt[eng] += 1
        ev = (eng, self.cnt[eng])
        if small:
            self.small_ev.add(ev)
        self.ops[eng].append((waits, fn, eng, 1))
        self._commit(ev, reads, writes)
        return ev

    def finalize(self):
        engs = set(self.ops.keys())
        waited = {e: set() for e in engs}
        for e in engs:
            for waits, fn, semname, inc in self.ops[e]:
                for s_, c_ in waits.items():
                    if s_ in engs:
                        waited[s_].add(c_)
        rank = {}
        for e in engs:
            rank[e] = {c: i + 1 for i, c in enumerate(sorted(waited[e]))}
        out = {e: [] for e in engs}
        for e in engs:
            idx = 0
            for waits, fn, semname, inc in self.ops[e]:
                w2 = {s_: (rank[s_][c_] if s_ in engs else c_) for s_, c_ in waits.items()}
                if semname == e:
                    idx += 1
                    out[e].append((w2, fn, semname, 1 if idx in waited[e] else 0))
                else:
                    out[e].append((w2, fn, semname, inc))
        self.ops = out

    def dma(self, eng, fn, sem, reads=(), writes=()):
        self._sem(sem, 16)
        waits = self._deps(eng, reads, writes)
        self.cnt[sem] += 16
        ev = (sem, self.cnt[sem])
        self.ops[eng].append((waits, fn, sem, 16))
        self._commit(ev, reads, writes)
        return ev

    def wait_all(self, eng, keys):
        waits = self._deps(eng, keys, ())
        self.ops[eng].append((waits, None, None, 0))


def build_program(nt=NT, depth=DEPTH, phases=("lru", "ret", "outp", "ffn")):
    nc = bass.Bass("TRN2", target_bir_lowering=False)
    S = Sched()
    rstage = 4
    rsub = int(os.environ.get("RSUB", "9"))
    for _p in phases:
        if _p.startswith("rstage"):
            rstage = int(_p[6:])

    x_in = nc.dram_tensor("x_in", [NT, 128, KC, T], F32, kind="ExternalInput").ap()
    y_out = nc.dram_tensor("y_out", [NT, 128, KC, T], F32, kind="ExternalOutput").ap()
    WBF = os.environ.get("WBF16", "0") == "1"
    WDT = BF16 if WBF else F32
    w_in = nc.dram_tensor("w_in", [DEPTH, 48, 128, KC, 128], WDT, kind="ExternalInput").ap()
    w_out = nc.dram_tensor("w_out", [DEPTH, 16, 128, KC, 128], WDT, kind="ExternalInput").ap()
    w_gate = nc.dram_tensor("w_gate", [DEPTH, 44, 128, KC, 128], WDT, kind="ExternalInput").ap()
    w_up = nc.dram_tensor("w_up", [DEPTH, 44, 128, KC, 128], WDT, kind="ExternalInput").ap()
    w_down = nc.dram_tensor("w_down", [DEPTH, FQ, 16, 128, FH, 128], WDT, kind="ExternalInput").ap()
    w_gates = nc.dram_tensor("w_gates", [DEPTH, 128, 16, 128], F32, kind="ExternalInput").ap()
    pvec_d = nc.dram_tensor("pvec", [128, PV_TOT], F32, kind="ExternalInput").ap()
    rot_d = nc.dram_tensor("rot", [NT, 128, 2, T], F32, kind="ExternalInput").ap()
    tab_d = nc.dram_tensor("tab", [128, 2, NH, 128], F32, kind="ExternalInput").ap()
    cmat_d = nc.dram_tensor("cmat", [128, 3, 128], F32, kind="ExternalInput").ap()
    csml_d = nc.dram_tensor("csml", [128, 16], F32, kind="ExternalInput").ap()

    import contextlib
    with contextlib.ExitStack() as es:
        def sb(name, shape, dt):
            return es.enter_context(nc.sbuf_tensor(name, shape, dt))

        xT = sb("xT", [128, KC, T], F32)
        bufA = sb("bufA", [128, KC, T], BF16)
        bufB = sb("bufB", [128, KC, T], BF16)
        wbuf = [sb("wbuf%d" % i, [128, KC, 128], BF16) for i in range(NB)]
        wg_sb = sb("wg_sb", [128, 16, 128], BF16)
        pvec = sb("pvec_sb", [128, PV_TOT], F32)
        rot = sb("rot_sb", [128, 2, T], BF16)
        tab = sb("tab_sb", [128, 2, NH, 128], BF16)
        cmat = sb("cmat_sb", [128, 3, 128], BF16)
        csml = sb("csml_sb", [128, 16], F32)
        rstd = sb("rstd", [128, T], F32)
        sqb = [sb("sqb%d" % i, [128, T], BF16) for i in range(2)]
        Sst = sb("Sst", [128, DEPTH, NH, 128], F32)
        hst = sb("hst", [128, DEPTH, 8], F32)
        tail = sb("tail", [128, DEPTH, 8, 3], F32)
        clv = sb("clv", [128, 8], F32)
        spt = [sb("spt%d" % i, [128, 8], F32) for i in range(4)]
        f5 = [sb("f5_%d" % i, [128, 512], F32) for i in range(8)]
        xbp = sb("xbp", [128, T + 3], F32)
        b1 = [sb("b1_%d" % i, [128, T], BF16) for i in range(5)]
        small = sb("small", [128, T], BF16)
        onall = small
        bnst = sb("bnst", [128, 8, 6], F32)
        bnag = sb("bnag", [128, 8, 2], F32)
        grs = sb("grs", [128, 8], F32)
        yacc = rstd

        psA = es.enter_context(nc.psum_tensor("psA", [128, 6, 512], F32))
        psT = es.enter_context(nc.psum_tensor("psT", [128, 2, 1024], BF16))

        sems = {}

        def getsem(name):
            if name not in sems:
                sems[name] = es.enter_context(nc.semaphore("s_" + name))
            return sems[name]

        def pv(col, n=1):
            return pvec[:, col:col + n]

        eps_ap = csml[:, 8:9]
        one_ap = csml[:, 9:10]
        ident = cmat[:, 0, :]
        ones_d = cmat[:, 1, :]
        ones_l = cmat[:, 2, :]

        plan = []
        for tile in range(nt):
            for l in range(depth):
                if "lru" in phases:
                    for c in range(8):
                        plan.append((w_in[l, 32 + c], KC))
                        plan.append((w_in[l, 40 + c], KC))
                if "ret" in phases:
                    for h in range(NH):
                        for base in ((8,), ((8,), (8, 16), (8, 16, 0))[min(rsub, 3) - 1] if rstage == 2 else (8, 16, 0), (8, 16, 0), (8, 16, 0, 24))[rstage - 1]:
                            plan.append((w_in[l, base + h], KC))
                if "outp" in phases:
                    for m in range(16):
                        plan.append((w_out[l, m], KC))
                if "ffn" in phases:
                    for half in range(FQ):
                        for m in range(FH):
                            plan.append((w_gate[l, half * FH + m], KC))
                            plan.append((w_up[l, half * FH + m], KC))
                        for m in range(16):
                            plan.append((w_down[l, half, m], FH))
        wstate = {"issued": 0, "next": 0}

        def w_issue(upto):
            upto = min(upto, len(plan))
            while wstate["issued"] < upto:
                n = wstate["issued"]
                src, kcs = plan[n]
                bi = n % NB
                dst = wbuf[bi][:, 0:kcs, :]
                S.dma("sp" if WBF else "pool", (lambda e, dst=dst, src=src: e.dma_start(out=dst, in_=src)),
                      "w%d" % bi, writes=[("w", bi)])
                wstate["issued"] += 1

        def w_next(kcs):
            n = wstate["next"]
            assert plan[n][1] == kcs
            w_issue(n + NB)
            wstate["next"] += 1
            bi = n % NB
            return wbuf[bi], ("wr", bi)

        pstate = {"bank": 0, "tslot": 0}

        def nextbank():
            b = pstate["bank"]
            pstate["bank"] = (b + 1) % 6
            return b

        def proj(rhs_buf, rhs_key, kcs, evac):
            wb, wkey = w_next(kcs)
            banks = (nextbank(), nextbank())

            def mm(e, wb=wb, banks=banks):
                ins = None
                for kc in range(kcs):
                    for th in range(2):
                        ins = e.matmul(psA[:, banks[th], :], lhsT=wb[:, kc, :],
                                       rhs=rhs_buf[:, kc, th * 512:(th + 1) * 512],
                                       start=(kc == 0), stop=(kc == kcs - 1))
                return ins
            S.op("pe", mm, reads=[wkey, rhs_key], writes=[("ps", banks[0]), ("ps", banks[1])])
            for th in range(2):
                evac(th, psA[:, banks[th], :], ("ps", banks[th]))

        def mslot():
            b = nextbank()
            return psA[:, b, 0:128], ("ps", b)

        def tslot():
            s = pstate["tslot"]
            pstate["tslot"] = (s + 1) % 2
            return psT[:, s, 0:512], ("ps", 6 + s)

        def rmsnorm(gcol, out_fn):
            nb = (nextbank(), nextbank())
            for kc in range(KC):
                sq = sqb[kc % 2]
                S.op("act", (lambda e, sq=sq, kc=kc: e.activation(out=sq[:, :], in_=xT[:, kc, :], func=AF.Square)),
                     reads=[("x", kc)], writes=[("sqb", kc % 2)])

                def mm(e, sq=sq, kc=kc):
                    ins = None
                    for th in range(2):
                        ins = e.matmul(psA[:, nb[th], :], lhsT=ones_d, rhs=sq[:, th * 512:(th + 1) * 512],
                                       start=(kc == 0), stop=(kc == KC - 1))
                    return ins
                S.op("pe", mm, reads=[("sqb", kc % 2), "cmat"], writes=[("ps", nb[0]), ("ps", nb[1])])
            for th in range(2):
                S.op("act", (lambda e, th=th: e.activation(out=rstd[:, th * 512:(th + 1) * 512], in_=psA[:, nb[th], :],
                                                           func=AF.Sqrt, bias=eps_ap, scale=1.0)),
                     reads=[("ps", nb[th]), "csml"], writes=[("rstd", th)])
                S.op("dve", (lambda e, th=th: e.reciprocal(out=rstd[:, th * 512:(th + 1) * 512],
                                                           in_=rstd[:, th * 512:(th + 1) * 512])),
                     reads=[("rstd", th)], writes=[("rstd", th)])
            for kc in range(KC):
                out_fn(kc, gcol + kc)

        def norm_to_bufA(kc, gc):
            S.op("dve", (lambda e, kc=kc, gc=gc: e.scalar_tensor_tensor(
                out=bufA[:, kc, :], in0=xT[:, kc, :], scalar=pv(gc), in1=rstd[:, :],
                op0=ALU.mult, op1=ALU.mult)),
                reads=[("x", kc), ("rstd", 0), ("rstd", 1), "pvec"], writes=[("A", kc)])

        def norm_inplace(kc, gc):
            S.op("dve", (lambda e, kc=kc, gc=gc: e.scalar_tensor_tensor(
                out=xT[:, kc, :], in0=xT[:, kc, :], scalar=pv(gc), in1=rstd[:, :],
                op0=ALU.mult, op1=ALU.mult)),
                reads=[("x", kc), ("rstd", 0), ("rstd", 1), "pvec"], writes=[("x", kc)])

        A_keys = [("A", kc) for kc in range(KC)]

        class AKey:
            pass

        S.dma("pool", lambda e: e.dma_start(out=pvec[:, :], in_=pvec_d[:, :]), "ld0", writes=["pvec"])
        S.dma("pool", lambda e: e.dma_start(out=tab[:, :, :, :], in_=tab_d[:, :, :, :]), "ld1", writes=["tab"])
        S.dma("pool", lambda e: e.dma_start(out=cmat[:, :, :], in_=cmat_d[:, :, :]), "ld2", writes=["cmat"])
        S.dma("pool", lambda e: e.dma_start(out=csml[:, :], in_=csml_d[:, :]), "ld3", writes=["csml"])
        S.op("dve", lambda e: e.memset(Sst[:, :, :, :], 0.0), writes=["Sst"])
        S.op("dve", lambda e: e.memset(hst[:, :, :], 0.0), writes=["hst"])
        S.op("dve", lambda e: e.memset(tail[:, :, :, :], 0.0), writes=["tail"])

        for tile in range(nt):
            S.dma("sp", (lambda e, tile=tile: e.dma_start(out=xT[:, :, :], in_=x_in[tile])), "ldx",
                  writes=[("x", kc) for kc in range(KC)])
            S.dma("pool", (lambda e, tile=tile: e.dma_start(out=rot[:, :, :], in_=rot_d[tile])), "ldr",
                  writes=["rot"])
            for l in range(depth):
                pb = PV_L * l
                S.dma("pool", (lambda e, l=l: e.dma_start(out=wg_sb[:, :, :], in_=w_gates[l])), "ldg",
                      writes=["wg"])
                lam = pv(pb + PV_LAM, 8)
                s0, s1, s2, s3 = spt
                S.op("dve", lambda e, lam=lam: e.tensor_scalar(out=s0[:, :], in0=lam, scalar1=-1.0, scalar2=None, op0=ALU.mult),
                     reads=["pvec"], writes=["s0"], small=True)
                S.op("dve", lambda e, lam=lam: e.tensor_tensor(out=s0[:, :], in0=s0[:, :], in1=lam, op=ALU.max),
                     reads=["pvec", "s0"], writes=["s0"], small=True)
                S.op("act", lambda e: e.activation(out=s0[:, :], in_=s0[:, :], func=AF.Exp, scale=-1.0),
                     reads=["s0"], writes=["s0"], small=True)
                S.op("dve", lambda e: e.tensor_scalar(out=s1[:, :], in0=s0[:, :], scalar1=2.0, scalar2=None, op0=ALU.add),
                     reads=["s0"], writes=["s1"], small=True)
                S.op("dve", lambda e: e.reciprocal(out=s1[:, :], in_=s1[:, :]), reads=["s1"], writes=["s1"], small=True)
                S.op("dve", lambda e: e.tensor_tensor(out=s1[:, :], in0=s1[:, :], in1=s0[:, :], op=ALU.mult),
                     reads=["s1", "s0"], writes=["s1"], small=True)
                S.op("dve", lambda e: e.tensor_tensor(out=s2[:, :], in0=s1[:, :], in1=s1[:, :], op=ALU.mult),
                     reads=["s1"], writes=["s2"], small=True)
                S.op("dve", lambda e: e.tensor_scalar(out=s3[:, :], in0=s2[:, :], scalar1=1.0 / 13.0, scalar2=1.0 / 11.0,
                                                      op0=ALU.mult, op1=ALU.add), reads=["s2"], writes=["s3"], small=True)
                for kk in (9.0, 7.0, 5.0, 3.0, 1.0):
                    S.op("dve", lambda e: e.tensor_tensor(out=s3[:, :], in0=s3[:, :], in1=s2[:, :], op=ALU.mult),
                         reads=["s3", "s2"], writes=["s3"], small=True)
                    S.op("dve", lambda e, kk=kk: e.tensor_scalar(out=s3[:, :], in0=s3[:, :], scalar1=1.0 / kk, scalar2=None,
                                                                 op0=ALU.add), reads=["s3"], writes=["s3"], small=True)
                S.op("dve", lambda e: e.tensor_tensor(out=s3[:, :], in0=s3[:, :], in1=s1[:, :], op=ALU.mult),
                     reads=["s3", "s1"], writes=["s3"], small=True)
                S.op("dve", lambda e, lam=lam: e.tensor_scalar(out=s0[:, :], in0=lam, scalar1=-1.0, scalar2=0.0,
                                                               op0=ALU.mult, op1=ALU.max), reads=["pvec", "s0"], writes=["s0"], small=True)
                S.op("dve", lambda e: e.scalar_tensor_tensor(out=s0[:, :], in0=s3[:, :], scalar=2.0, in1=s0[:, :],
                                                             op0=ALU.mult, op1=ALU.add), reads=["s3", "s0"], writes=["s0"], small=True)
                S.op("dve", lambda e: e.tensor_scalar(out=clv[:, :], in0=s0[:, :], scalar1=-8.0, scalar2=None, op0=ALU.mult),
                     reads=["s0"], writes=["clv"], small=True)

                rmsnorm(pb + PV_N1G, norm_to_bufA)

                if "lru" not in phases:
                    S.op("dve", lambda e: e.memset(bufB[:, 8:16, :], 0.0), writes=[("B", m_, t_) for m_ in range(8, 16) for t_ in range(2)])
                for c in (range(8) if "lru" in phases else ()):
                    S.op("act", (lambda e, c=c, l=l: e.activation(out=xbp[:, 0:3], in_=tail[:, l, c, :], func=AF.Copy)),
                         reads=["tail"], writes=[("xbp", -1)], small=True)

                    def ev_xb(th, ps, pk, c=c):
                        S.op("act", (lambda e, th=th, ps=ps: e.activation(out=xbp[:, 3 + th * 512:3 + (th + 1) * 512],
                                                                          in_=ps, func=AF.Copy)),
                             reads=[pk], writes=[("xbp", th)])
                    proj(bufA, "Aall", KC, ev_xb)
                    S.op("act", (lambda e, c=c, l=l: e.activation(out=tail[:, l, c, :], in_=xbp[:, T:T + 3], func=AF.Copy)),
                         reads=[("xbp", 1)], writes=["tail"], small=True)

                    def ev_yb(th, ps, pk, c=c, l=l, pb=pb):
                        xc, t1, t2, t3 = f5[4 * th + 0], f5[4 * th + 1], f5[4 * th + 2], f5[4 * th + 3]
                        kx, k1, k2, k3 = [("f5", 4 * th + i) for i in range(4)]
                        xcb = b1[th]
                        o0 = th * 512
                        cw = pb + PV_CW
                        xdeps = [("xbp", -1), ("xbp", 0), ("xbp", 1)]
                        S.op("act", (lambda e: e.activation(out=xc[:, :], in_=xbp[:, o0 + 3:o0 + 515], func=AF.Identity,
                                                            bias=pv(pb + PV_CB + c), scale=pv(cw + 3 * 8 + c))),
                             reads=xdeps + ["pvec"], writes=[kx])
                        for j in range(3):
                            S.op("dve", (lambda e, j=j: e.scalar_tensor_tensor(
                                out=xc[:, :], in0=xbp[:, o0 + j:o0 + j + 512], scalar=pv(cw + j * 8 + c), in1=xc[:, :],
                                op0=ALU.mult, op1=ALU.add)), reads=xdeps + [kx, "pvec"], writes=[kx])
                        S.op("act", (lambda e: e.activation(out=xcb[:, 0:512], in_=xc[:, :], func=AF.Copy)),
                             reads=[kx], writes=[("b1", th)])
                        rb_, ib_ = nextbank(), nextbank()
                        rps = psA[:, rb_, :]
                        ips = psA[:, ib_, :]
                        rk, ik = ("ps", rb_), ("ps", ib_)
                        S.op("pe", (lambda e: e.matmul(rps, lhsT=wg_sb[:, c, :], rhs=xcb[:, 0:512], start=True, stop=True)),
                             reads=[("b1", th), "wg"], writes=[rk])
                        S.op("pe", (lambda e: e.matmul(ips, lhsT=wg_sb[:, 8 + c, :], rhs=xcb[:, 0:512], start=True, stop=True)),
                             reads=[("b1", th), "wg"], writes=[ik])
                        S.op("act", (lambda e: e.activation(out=t1[:, :], in_=rps, func=AF.Sigmoid, bias=pv(pb + PV_BA + c))),
                             reads=[rk, "pvec"], writes=[k1])
                        S.op("act", (lambda e: e.activation(out=t1[:, :], in_=t1[:, :], func=AF.Exp, scale=clv[:, c:c + 1])),
                             reads=[k1, "clv"], writes=[k1])
                        S.op("act", (lambda e: e.activation(out=t3[:, :], in_=ips, func=AF.Sigmoid, bias=pv(pb + PV_BX + c))),
                             reads=[ik, "pvec"], writes=[k3])
                        S.op("dve", (lambda e: e.tensor_tensor(out=t2[:, :], in0=t1[:, :], in1=t1[:, :], op=ALU.mult)),
                             reads=[k1], writes=[k2])
                        S.op("act", (lambda e: e.activation(out=t2[:, :], in_=t2[:, :], func=AF.Sqrt, bias=one_ap, scale=-1.0)),
                             reads=[k2, "csml"], writes=[k2])
                        S.op("dve", (lambda e: e.tensor_tensor(out=t3[:, :], in0=t3[:, :], in1=xc[:, :], op=ALU.mult)),
                             reads=[k3, kx], writes=[k3])
                        S.op("dve", (lambda e: e.tensor_tensor(out=t3[:, :], in0=t3[:, :], in1=t2[:, :], op=ALU.mult)),
                             reads=[k3, k2], writes=[k3])
                        S.op("dve", (lambda e: e.tensor_tensor_scan(out=t2[:, :], data0=t1[:, :], data1=t3[:, :],
                                                                    initial=hst[:, l, c:c + 1], op0=ALU.mult, op1=ALU.add)),
                             reads=[k1, k3, "hst", k2], writes=[k2])
                        S.op("dve", (lambda e: e.tensor_copy(out=hst[:, l, c:c + 1], in_=t2[:, 511:512])),
                             reads=[k2], writes=["hst"], small=True)
                        S.op("act", (lambda e: e.activation(out=t3[:, :], in_=ps, func=AF.Square)),
                             reads=[pk, k3], writes=[k3])
                        S.op("dve", (lambda e: e.tensor_scalar(out=t3[:, :], in0=t3[:, :], scalar1=0.044715, scalar2=1.0,
                                                               op0=ALU.mult, op1=ALU.add)), reads=[k3], writes=[k3])
                        S.op("dve", (lambda e: e.tensor_tensor(out=t3[:, :], in0=ps, in1=t3[:, :], op=ALU.mult)),
                             reads=[pk, k3], writes=[k3])
                        S.op("act", (lambda e: e.activation(out=t3[:, :], in_=t3[:, :], func=AF.Sigmoid, scale=1.5957691216057308)),
                             reads=[k3], writes=[k3])
                        S.op("dve", (lambda e: e.tensor_tensor(out=t3[:, :], in0=ps, in1=t3[:, :], op=ALU.mult)),
                             reads=[pk, k3], writes=[k3])
                        S.op("dve", (lambda e: e.tensor_tensor(out=t2[:, :], in0=t2[:, :], in1=t3[:, :], op=ALU.mult)),
                             reads=[k2, k3], writes=[k2])
                        S.op("act", (lambda e: e.activation(out=t3[:, :], in_=t2[:, :], func=AF.Square)),
                             reads=[k2], writes=[k3])
                        if c == 0:
                            S.op("dve", (lambda e: e.tensor_copy(out=yacc[:, o0:o0 + 512], in_=t3[:, :])),
                                 reads=[k3], writes=[("rstd", th)])
                        else:
                            S.op("dve", (lambda e: e.tensor_tensor(out=yacc[:, o0:o0 + 512], in0=yacc[:, o0:o0 + 512],
                                                                   in1=t3[:, :], op=ALU.add)),
                                 reads=[k3, ("rstd", th)], writes=[("rstd", th)])
                        S.op("act", (lambda e: e.activation(out=bufB[:, 8 + c, o0:o0 + 512], in_=t2[:, :], func=AF.Identity,
                                                            scale=pv(pb + PV_LNG + c))),
                             reads=[k2, "pvec"], writes=[("B", 8 + c, th)])
                    proj(bufA, "Aall", KC, ev_yb)
                for th in (range(2) if "lru" in phases else ()):
                    o0 = th * 512
                    lb_ = nextbank()
                    S.op("act", (lambda e, th=th, o0=o0: e.activation(out=sqb[0][:, o0:o0 + 512], in_=yacc[:, o0:o0 + 512], func=AF.Copy)),
                         reads=[("rstd", th)], writes=[("sqb", 0)])
                    S.op("pe", (lambda e, th=th, o0=o0, lb_=lb_: e.matmul(psA[:, lb_, :], lhsT=ones_l, rhs=sqb[0][:, o0:o0 + 512],
                                                                           start=True, stop=True)),
                         reads=[("sqb", 0), "cmat"], writes=[("ps", lb_)])
                    S.op("act", (lambda e, th=th, o0=o0, lb_=lb_: e.activation(out=rstd[:, o0:o0 + 512], in_=psA[:, lb_, :],
                                                                               func=AF.Sqrt, bias=eps_ap, scale=1.0)),
                         reads=[("ps", lb_), "csml"], writes=[("rstd", th)])
                    S.op("dve", (lambda e, o0=o0: e.reciprocal(out=rstd[:, o0:o0 + 512], in_=rstd[:, o0:o0 + 512])),
                         reads=[("rstd", th)], writes=[("rstd", th)])
                for c in (range(8) if "lru" in phases else ()):
                    S.op("dve", (lambda e, c=c: e.tensor_tensor(out=bufB[:, 8 + c, :], in0=bufB[:, 8 + c, :], in1=rstd[:, :],
                                                                op=ALU.mult)),
                         reads=[("B", 8 + c, 0), ("B", 8 + c, 1), ("rstd", 0), ("rstd", 1)],
                         writes=[("B", 8 + c, 0), ("B", 8 + c, 1)])

                kT, vT, qT, ktok, vtok = b1
                qd = vT
                gsil = sqb[1]
                rt1 = [f5[0], f5[4]]
                rt2 = [f5[1], f5[5]]
                rk1 = [("f5", 0), ("f5", 4)]
                rk2 = [("f5", 1), ("f5", 5)]

                def rotary_evac(dst, dkey):
                    def ev(th, ps, pk):
                        o0 = th * 512
                        t1, t2, k1, k2 = rt1[th], rt2[th], rk1[th], rk2[th]
                        S.op("dve", (lambda e: e.tensor_tensor(out=t1[:, :], in0=ps, in1=rot[:, 0, o0:o0 + 512], op=ALU.mult)),
                             reads=[pk, "rot"], writes=[k1])
                        S.op("dve", (lambda e: e.tensor_tensor(out=t2[0:64, :], in0=ps[64:128, :], in1=rot[0:64, 1, o0:o0 + 512],
                                                               op=ALU.mult)), reads=[pk, "rot"], writes=[k2])
                        S.op("dve", (lambda e: e.tensor_tensor(out=t2[64:128, :], in0=ps[0:64, :], in1=rot[64:128, 1, o0:o0 + 512],
                                                               op=ALU.mult)), reads=[pk, "rot", k2], writes=[k2])
                        S.op("dve", (lambda e: e.tensor_tensor(out=dst[:, o0:o0 + 512], in0=t1[:, :], in1=t2[:, :], op=ALU.add)),
                             reads=[k1, k2], writes=[("b1", dkey, th)])
                    return ev

                if "ret" not in phases:
                    S.op("dve", lambda e: e.memset(bufB[:, 0:8, :], 0.0), writes=[("B", m_, t_) for m_ in range(8) for t_ in range(2)])
                for h in (range(NH) if "ret" in phases else ()):
                    gam = 1.0 - 2.0 ** (-5.0 - h)
                    proj(bufA, "Aall", KC, rotary_evac(kT, 0))
                    if rstage < 2:
                        S.op("dve", lambda e, h=h: e.memset(bufB[:, h, :], 0.0), writes=[("B", h, 0), ("B", h, 1)])
                        continue
                    for g4 in range(2):
                        tp, tk = tslot()

                        def trk(e, g4=g4, tp=tp):
                            ins = None
                            for jj in range(4):
                                j = g4 * 4 + jj
                                ins = e.transpose(tp[:, jj * 128:(jj + 1) * 128], kT[:, j * 128:(j + 1) * 128], ident)
                            return ins
                        S.op("pe", trk, reads=[("b1", 0, g4), "cmat"], writes=[tk])
                        S.op("act", (lambda e, g4=g4, tp=tp, h=h: e.activation(out=ktok[:, g4 * 512:(g4 + 1) * 512], in_=tp,
                                                                               func=AF.Identity, scale=csml[:, h:h + 1])),
                             reads=[tk, "csml"], writes=[("b1", 3, g4)])

                    if rstage == 2 and rsub < 2:
                        S.op("dve", lambda e, h=h: e.memset(bufB[:, h, :], 0.0), writes=[("B", h, 0), ("B", h, 1)])
                        continue
                    def ev_v(th, ps, pk):
                        S.op("act", (lambda e: e.activation(out=vT[:, th * 512:(th + 1) * 512], in_=ps, func=AF.Copy)),
                             reads=[pk], writes=[("b1", 1, th)])
                    proj(bufA, "Aall", KC, ev_v)
                    for g4 in range(2):
                        tp, tk = tslot()

                        def trv(e, g4=g4, tp=tp):
                            ins = None
                            for jj in range(4):
                                j = g4 * 4 + jj
                                ins = e.transpose(tp[:, jj * 128:(jj + 1) * 128], vT[:, j * 128:(j + 1) * 128], ident)
                            return ins
                        S.op("pe", trv, reads=[("b1", 1, g4), "cmat"], writes=[tk])
                        S.op("act", (lambda e, g4=g4, tp=tp: e.activation(out=vtok[:, g4 * 512:(g4 + 1) * 512], in_=tp, func=AF.Copy)),
                             reads=[tk], writes=[("b1", 4, g4)])
                    if rstage == 2 and rsub < 3:
                        S.op("dve", lambda e, h=h: e.memset(bufB[:, h, :], 0.0), writes=[("B", h, 0), ("B", h, 1)])
                        continue
                    proj(bufA, "Aall", KC, rotary_evac(qT, 2))
                    for j in range(8):
                        S.op("dve", (lambda e, j=j, h=h: e.tensor_tensor(out=qd[:, j * 128:(j + 1) * 128],
                                                                         in0=qT[:, j * 128:(j + 1) * 128],
                                                                         in1=tab[:, 1, h, :], op=ALU.mult)),
                             reads=[("b1", 2, j // 4), "tab"], writes=[("b1", 1, j // 4)])
                    if rstage < 3:
                        S.op("dve", lambda e, h=h: e.memset(bufB[:, h, :], 0.0), writes=[("B", h, 0), ("B", h, 1)])
                        continue
                    g128 = float(gam ** 128)
                    ub = (nextbank(), nextbank())

                    def mmu(e, ub=ub):
                        ins = None
                        for j in range(8):
                            js = slice(j * 128, (j + 1) * 128)
                            ins = e.matmul(psA[:, ub[j // 4], (j % 4) * 128:(j % 4 + 1) * 128], lhsT=ktok[:, js], rhs=vtok[:, js],
                                           start=True, stop=True)
                        return ins
                    S.op("pe", mmu, reads=[("b1", 3, 0), ("b1", 3, 1), ("b1", 4, 0), ("b1", 4, 1)],
                         writes=[("ps", ub[0]), ("ps", ub[1])])
                    if rstage == 3 and rsub < 2:
                        S.op("dve", lambda e, h=h: e.memset(bufB[:, h, :], 0.0), writes=[("B", h, 0), ("B", h, 1)])
                        continue
                    sbk = (nextbank(), nextbank())

                    def mms(e, sbk=sbk):
                        ins = None
                        for j in range(8):
                            js = slice(j * 128, (j + 1) * 128)
                            ins = e.matmul(psA[:, sbk[j // 4], (j % 4) * 128:(j % 4 + 1) * 128], lhsT=kT[:, js], rhs=qT[:, js],
                                           start=True, stop=True)
                        return ins
                    S.op("pe", mms, reads=[("b1", 0, 0), ("b1", 0, 1), ("b1", 2, 0), ("b1", 2, 1)],
                         writes=[("ps", sbk[0]), ("ps", sbk[1])])
                    for j in range(8):
                        S.op("dve", (lambda e, j=j, h=h, sbk=sbk: e.tensor_tensor(
                            out=small[:, j * 128:(j + 1) * 128], in0=psA[:, sbk[j // 4], (j % 4) * 128:(j % 4 + 1) * 128],
                            in1=tab[:, 0, h, :], op=ALU.mult)),
                            reads=[("ps", sbk[j // 4]), "tab"], writes=[("small", j // 4)])
                    if rstage == 3 and rsub < 3:
                        S.op("dve", lambda e, h=h: e.memset(bufB[:, h, :], 0.0), writes=[("B", h, 0), ("B", h, 1)])
                        continue
                    def sslot(j):
                        return f5[6 + j // 4][:, (j % 4) * 128:(j % 4 + 1) * 128]
                    S.op("act", (lambda e, h=h, l=l: e.activation(out=sslot(0), in_=Sst[:, l, h, :], func=AF.Copy)),
                         reads=["Sst"], writes=[("f5", 6)])
                    for j in range(8):
                        dst = sslot(j + 1) if j < 7 else Sst[:, l, h, :]
                        S.op("dve", (lambda e, j=j, dst=dst, ub=ub, g128=g128: e.scalar_tensor_tensor(
                            out=dst, in0=sslot(j), scalar=g128,
                            in1=psA[:, ub[j // 4], (j % 4) * 128:(j % 4 + 1) * 128],
                            op0=ALU.mult, op1=ALU.add)),
                            reads=[("ps", ub[j // 4]), ("f5", 6 + j // 4)],
                            writes=[("f5", 6 + (j + 1) // 4)] if j < 7 else ["Sst"], small=True)
                    for g4 in range(2):
                        S.op("act", (lambda e, g4=g4: e.activation(out=kT[:, g4 * 512:(g4 + 1) * 512], in_=f5[6 + g4][:, :], func=AF.Copy)),
                             reads=[("f5", 6 + g4)], writes=[("b1", 0, g4)])
                    if rstage == 3 and rsub < 4:
                        S.op("dve", lambda e, h=h: e.memset(bufB[:, h, :], 0.0), writes=[("B", h, 0), ("B", h, 1)])
                        continue
                    ob1 = (nextbank(), nextbank())

                    def mmo1(e, ob1=ob1):
                        ins = None
                        for j in range(8):
                            js = slice(j * 128, (j + 1) * 128)
                            ins = e.matmul(psA[:, ob1[j // 4], (j % 4) * 128:(j % 4 + 1) * 128], lhsT=small[:, js], rhs=vtok[:, js],
                                           start=True, stop=True)
                        return ins
                    S.op("pe", mmo1, reads=[("small", 0), ("small", 1), ("b1", 4, 0), ("b1", 4, 1)],
                         writes=[("ps", ob1[0]), ("ps", ob1[1])])
                    ob2 = (nextbank(), nextbank())

                    def mmo2(e, ob2=ob2):
                        ins = None
                        for j in range(8):
                            js = slice(j * 128, (j + 1) * 128)
                            ins = e.matmul(psA[:, ob2[j // 4], (j % 4) * 128:(j % 4 + 1) * 128], lhsT=qd[:, js], rhs=kT[:, js],
                                           start=True, stop=True)
                        return ins
                    S.op("pe", mmo2, reads=[("b1", 1, 0), ("b1", 1, 1), ("b1", 0, 0), ("b1", 0, 1)],
                         writes=[("ps", ob2[0]), ("ps", ob2[1])])
                    for g4 in range(2):
                        S.op("act", (lambda e, g4=g4, ob1=ob1: e.activation(out=f5[2 + g4][:, :], in_=psA[:, ob1[g4], :], func=AF.Copy)),
                             reads=[("ps", ob1[g4])], writes=[("f5", 2 + g4)])
                        S.op("dve", (lambda e, g4=g4, ob2=ob2: e.tensor_tensor(out=f5[2 + g4][:, :], in0=psA[:, ob2[g4], :],
                                                                               in1=f5[2 + g4][:, :], op=ALU.add)),
                             reads=[("ps", ob2[g4]), ("f5", 2 + g4)], writes=[("f5", 2 + g4)])
                    if rstage == 3 and rsub < 5:
                        S.op("dve", lambda e, h=h: e.memset(bufB[:, h, :], 0.0), writes=[("B", h, 0), ("B", h, 1)])
                        continue
                    for j in range(8):
                        S.op("dve", (lambda e, j=j: e.bn_stats(out=bnst[:, j, :],
                                                               in_=f5[2 + j // 4][:, (j % 4) * 128:(j % 4 + 1) * 128])),
                             reads=[("f5", 2 + j // 4)], writes=[("bnst", j)])
                    for j in range(8):
                        S.op("dve", (lambda e, j=j: e.bn_aggr(out=bnag[:, j, :], in_=bnst[:, j, :])),
                             reads=[("bnst", j)], writes=[("bnag", j)], small=True)
                    if rstage < 4:
                        S.op("dve", lambda e, h=h: e.memset(bufB[:, h, :], 0.0), writes=[("B", h, 0), ("B", h, 1)])
                        continue
                    def ev_g(th, ps, pk):
                        tg, tgk = f5[6 + th], ("f5", 6 + th)
                        S.op("act", (lambda e: e.activation(out=tg[:, :], in_=ps, func=AF.Sigmoid)),
                             reads=[pk], writes=[tgk])
                        S.op("dve", (lambda e: e.tensor_tensor(out=gsil[:, th * 512:(th + 1) * 512], in0=ps, in1=tg[:, :], op=ALU.mult)),
                             reads=[pk, tgk], writes=[("sqb", 1, th)])
                    proj(bufA, "Aall", KC, ev_g)
                    bkeys = [("bnag", j) for j in range(8)]
                    S.op("act", (lambda e: e.activation(out=grs[:, :], in_=bnag[:, :, 1], func=AF.Sqrt, bias=eps_ap, scale=1.0)),
                         reads=bkeys + ["csml"], writes=["grs"], small=True)
                    S.op("dve", (lambda e: e.reciprocal(out=grs[:, :], in_=grs[:, :])), reads=["grs"], writes=["grs"], small=True)
                    for j in range(8):
                        S.op("dve", (lambda e, j=j: e.tensor_scalar(out=onall[:, j * 128:(j + 1) * 128],
                                                                    in0=f5[2 + j // 4][:, (j % 4) * 128:(j % 4 + 1) * 128],
                                                                    scalar1=bnag[:, j, 0:1], scalar2=grs[:, j:j + 1],
                                                                    op0=ALU.subtract, op1=ALU.mult)),
                             reads=[("f5", 2 + j // 4), ("bnag", j), "grs"], writes=[("small", j // 4)])
                    for g4 in range(2):
                        tp, tk = tslot()

                        def trn(e, g4=g4, tp=tp):
                            ins = None
                            for jj in range(4):
                                j = g4 * 4 + jj
                                ins = e.transpose(tp[:, jj * 128:(jj + 1) * 128], onall[:, j * 128:(j + 1) * 128], ident)
                            return ins
                        S.op("pe", trn, reads=[("small", g4), "cmat"], writes=[tk])
                        S.op("dve", (lambda e, g4=g4, tp=tp, h=h, pb=pb: e.scalar_tensor_tensor(
                            out=bufB[:, h, g4 * 512:(g4 + 1) * 512], in0=tp, scalar=pv(pb + PV_GNG + h),
                            in1=gsil[:, g4 * 512:(g4 + 1) * 512], op0=ALU.mult, op1=ALU.mult)),
                            reads=[tk, ("sqb", 1, g4), "pvec"], writes=[("B", h, g4)])

                def ev_res(mo):
                    def ev(th, ps, pk):
                        o0 = th * 512
                        S.op("dve", (lambda e: e.tensor_tensor(out=xT[:, mo, o0:o0 + 512], in0=ps, in1=xT[:, mo, o0:o0 + 512],
                                                               op=ALU.add)), reads=[pk, ("x", mo)], writes=[("x", mo)])
                    return ev
                for mo in (range(16) if "outp" in phases else ()):
                    proj(bufB, "Ball", KC, ev_res(mo))

                if "ffn" in phases:
                    rmsnorm(pb + PV_N2G, norm_to_bufA)
                for half in (range(FQ) if "ffn" in phases else ()):
                    for m in range(FH):
                        def ev_gate(th, ps, pk):
                            S.op("act", (lambda e: e.activation(out=f5[2 + 4 * th][:, :], in_=ps, func=AF.Sigmoid)),
                                 reads=[pk], writes=[("f5", 2 + 4 * th)])
                            S.op("dve", (lambda e: e.tensor_tensor(out=f5[2 + 4 * th][:, :], in0=ps, in1=f5[2 + 4 * th][:, :], op=ALU.mult)),
                                 reads=[pk, ("f5", 2 + 4 * th)], writes=[("f5", 2 + 4 * th)])
                        proj(bufA, "Aall", KC, ev_gate)

                        def ev_up(th, ps, pk, m=m):
                            o0 = th * 512
                            S.op("dve", (lambda e: e.tensor_tensor(out=bufB[:, m, o0:o0 + 512], in0=ps, in1=f5[2 + 4 * th][:, :],
                                                                   op=ALU.mult)), reads=[pk, ("f5", 2 + 4 * th)], writes=[("B", m, th)])
                        proj(bufA, "Aall", KC, ev_up)
                    for mo in range(16):
                        proj(bufB, "Ball", FH, ev_res(mo))

            rmsnorm(PV_FINAL, norm_inplace)
            S.dma("sp", (lambda e, tile=tile: e.dma_start(out=y_out[tile], in_=xT[:, :, :])), "st",
                  reads=[("x", kc) for kc in range(KC)])
        S.wait_all("sp", [])
        final_st = S.cnt["st"]

        engmap = {"pe": "tensor", "act": "scalar", "dve": "vector", "pool": "gpsimd", "sp": "sync"}
        if os.environ.get("DENSE", "0") != "1":
            S.finalize()
        for _n in list(S.cnt.keys()):
            getsem(_n)
        with nc.Block() as block:
            def make(engname):
                def body(e):
                    for waits, fn, semname, inc in S.ops[engname]:
                        for sname, val in waits.items():
                            e.wait_ge(getsem(sname), val)
                        if fn is None:
                            continue
                        ins = fn(e)
                        if inc:
                            ins.then_inc(getsem(semname), inc)
                    if engname == "sp":
                        e.wait_ge(getsem("st"), final_st)
                return body
            for engname, attr in engmap.items():
                getattr(block, attr)(make(engname))
    return nc


_orig_deps = Sched._deps
_orig_commit = Sched._commit


def _expand(keys):
    out = []
    for k in keys:
        if k == "Aall":
            out.extend(("A", kc) for kc in range(KC))
        elif isinstance(k, tuple) and k[0] == "wr":
            out.append(("w", k[1]))
        elif isinstance(k, tuple) and k[0] == "pmb":
            out.extend(("pm", (k[1] - 4) * 4 + i) for i in range(4))
        elif isinstance(k, tuple) and k[0] == "sqb" and len(k) == 2:
            out.extend([("sqb", k[1], 0), ("sqb", k[1], 1)])
        elif isinstance(k, tuple) and k[0] == "b1" and len(k) == 2:
            out.extend([("b1", k[1], 0), ("b1", k[1], 1)])
        elif k == "Ball":
            out.extend(("B", m, th) for m in range(KC) for th in range(2))
        else:
            out.append(k)
    return out


def _deps2(self, eng, reads, writes, small=False):
    return _orig_deps(self, eng, _expand(reads), _expand(writes), small)


def _commit2(self, ev, reads, writes):
    return _orig_commit(self, ev, _expand(reads), _expand(writes))


Sched._deps = _deps2
Sched._commit = _commit2


def _tile_w(w, kcs):
    K, N = w.shape
    assert K == kcs * 128
    return np.ascontiguousarray(w.reshape(kcs, 128, N // 128, 128).transpose(2, 1, 0, 3))


def _host_tables():
    f32 = np.float32
    H, C = NH, CH
    log_g = np.log1p(-np.exp2(-5.0 - np.arange(H, dtype=np.float64)))
    idx = np.arange(128, dtype=np.float64)
    a = idx[None, :]
    c = idx[:, None]
    same_or_prev = (np.floor(c / C) <= np.floor(a / C))
    mask = np.exp(log_g[:, None, None] * np.abs(a - c)[None]) * same_or_prev[None] * (DK ** -0.5)
    qdec = np.exp(log_g[:, None] * (idx + 1.0)[None, :])
    kdec = np.exp(log_g[:, None] * (127.0 - idx)[None, :]) * (DK ** -0.5)
    tab = np.zeros((128, 2, H, 128), f32)
    tab[:, 0] = mask.transpose(1, 0, 2)
    tab[:, 1] = np.broadcast_to(qdec[None], (128, H, 128))
    cmat = np.zeros((128, 3, 128), f32)
    cmat[:, 0] = np.eye(128)
    cmat[:, 1] = 1.0 / D
    cmat[:, 2] = 1.0 / 1024.0
    csml = np.zeros((128, 16), f32)
    csml[:, 0:8] = kdec.T
    csml[:, 8] = EPS
    csml[:, 9] = 1.0
    inv = 1.0 / (10000.0 ** (np.arange(0, DK, 2, dtype=np.float64) / DK))
    rot = np.zeros((NT, 128, 2, T), f32)
    for t in range(NT):
        pos = np.arange(t * T, (t + 1) * T, dtype=np.float64)
        ang = (pos[None, :].astype(np.float32) * inv[:, None].astype(np.float32)).astype(np.float32)
        cs, sn = np.cos(ang.astype(np.float64)), np.sin(ang.astype(np.float64))
        rot[t, 0:64, 0] = cs
        rot[t, 64:128, 0] = cs
        rot[t, 0:64, 1] = -sn
        rot[t, 64:128, 1] = sn
    return tab, cmat, csml, rot


def prep(x, norm1_g, w_in, ret_gn_g, lru_conv_w, lru_conv_b, lru_wa, lru_ba, lru_wx, lru_bx,
         lru_lambda, lru_norm_g, w_out, norm2_g, ffn_w_gate, ffn_w_up, ffn_w_down, final_g):
    f32 = np.float32
    x = np.asarray(x, f32)
    w_in_t = np.stack([_tile_w(np.asarray(w_in[l], f32), KC) for l in range(DEPTH)])
    w_out_t = np.stack([_tile_w(np.asarray(w_out[l], f32), KC) for l in range(DEPTH)])
    w_gate_t = np.stack([_tile_w(np.asarray(ffn_w_gate[l], f32), KC) for l in range(DEPTH)])
    w_up_t = np.stack([_tile_w(np.asarray(ffn_w_up[l], f32), KC) for l in range(DEPTH)])
    wd = np.asarray(ffn_w_down, f32)
    w_down_t = np.stack([np.stack([_tile_w(wd[l, hf * FH * 128:(hf + 1) * FH * 128], FH) for hf in range(FQ)])
                         for l in range(DEPTH)])
    wgates = np.zeros((DEPTH, 128, 16, 128), f32)
    for l in range(DEPTH):
        for gi, wsrc in enumerate((lru_wa, lru_wx)):
            ws = np.asarray(wsrc[l], f32)
            for c in range(8):
                wgates[l, 0:64, gi * 8 + c, 0:64] = ws[2 * c]
                wgates[l, 64:128, gi * 8 + c, 64:128] = ws[2 * c + 1]
    pvec = np.zeros((128, PV_TOT), f32)

    def colmaj(v, n):
        return np.asarray(v, f32).reshape(n, 128).T

    for l in range(DEPTH):
        pb = PV_L * l
        pvec[:, pb + PV_N1G:pb + PV_N1G + 16] = colmaj(norm1_g[l], 16)
        pvec[:, pb + PV_N2G:pb + PV_N2G + 16] = colmaj(norm2_g[l], 16)
        pvec[:, pb + PV_GNG:pb + PV_GNG + 8] = colmaj(ret_gn_g[l], 8)
        for j in range(4):
            pvec[:, pb + PV_CW + j * 8:pb + PV_CW + (j + 1) * 8] = colmaj(lru_conv_w[l][j], 8)
        pvec[:, pb + PV_CB:pb + PV_CB + 8] = colmaj(lru_conv_b[l], 8)
        pvec[:, pb + PV_BA:pb + PV_BA + 8] = colmaj(lru_ba[l], 8)
        pvec[:, pb + PV_BX:pb + PV_BX + 8] = colmaj(lru_bx[l], 8)
        pvec[:, pb + PV_LAM:pb + PV_LAM + 8] = colmaj(lru_lambda[l], 8)
        pvec[:, pb + PV_LNG:pb + PV_LNG + 8] = colmaj(lru_norm_g[l], 8)
    pvec[:, PV_FINAL:PV_FINAL + 16] = colmaj(final_g, 16)
    tab, cmat, csml, rot = _host_tables()

    def x_tiles(b):
        xb_ = x[b]
        return np.ascontiguousarray(xb_.reshape(NT, T, KC, 128).transpose(0, 3, 2, 1))

    shared = dict(w_in=w_in_t, w_out=w_out_t, w_gate=w_gate_t, w_up=w_up_t, w_down=w_down_t,
                  w_gates=wgates, pvec=pvec, rot=rot, tab=tab, cmat=cmat, csml=csml)
    return shared, x_tiles


def kernel(**inputs):
    f32 = np.float32
    shared, x_tiles = prep(**inputs)
    B = np.asarray(inputs["x"]).shape[0]
    busy = [0, 1, 4, 5]
    zeros = {k: np.zeros_like(v) for k, v in shared.items()}
    zx = np.zeros((NT, 128, KC, T), f32)
    in_maps = []
    for c in range(NCORES):
        if c in busy:
            m = dict(shared)
            m["x_in"] = x_tiles(busy.index(c))
        else:
            m = dict(zeros)
            m["x_in"] = zx
        in_maps.append(m)
    nc = build_program()
    res = run_bass_kernel_spmd(nc, in_maps, core_ids=list(range(NCORES)))
    out = np.zeros((B, NT * T, D), f32)
    for b in range(B):
        yt = np.asarray(res.results[busy[b]]["y_out"], f32)
        out[b] = yt.transpose(0, 3, 2, 1).reshape(NT * T, D)
    return out
```

```python
import os
import numpy as np
import concourse.bass as bass
import concourse.mybir as mybir
from concourse.bass_utils import run_bass_kernel_spmd

F32 = mybir.dt.float32
BF16 = mybir.dt.bfloat16
AF = mybir.ActivationFunctionType
ALU = mybir.AluOpType

D = 2048
KC = 16
T = 1024
NT = 2
DEPTH = 2
NH = 8
DK = 128
CH = 64
DFF = 5632
FH = 11
FQ = 4
NB = 4
EPS = 1e-6
NCORES = 8

PV_N1G, PV_N2G, PV_GNG, PV_CW, PV_CB, PV_BA, PV_BX, PV_LAM, PV_LNG = 0, 16, 32, 40, 72, 80, 88, 96, 104
PV_L = 112
PV_FINAL = PV_L * DEPTH
PV_TOT = PV_FINAL + 16


class Sched:
    def __init__(self):
        self.ops = {e: [] for e in ("pe", "act", "dve", "pool", "sp")}
        self.cnt = {}
        self.step = {}
        self.seen = {e: {} for e in self.ops}
        self.lastw = {}
        self.readers = {}
        self.small_ev = set()

    def _sem(self, name, step):
        if name not in self.cnt:
            self.cnt[name] = 0
            self.step[name] = step

    def _deps(self, eng, reads, writes, small=False):
        waits = {}

        def need(ev, war=False):
            s, c = ev
            if s == eng and eng == "pe":
                return
            if c > self.seen[eng].get(s, 0):
                waits[s] = max(waits.get(s, 0), c)

        for k in reads:
            if k in self.lastw:
                need(self.lastw[k])
        for k in writes:
            if k in self.lastw:
                need(self.lastw[k])
            for r in self.readers.get(k, ()):
                need(r, war=True)
        for s, c in waits.items():
            self.seen[eng][s] = c
        return waits

    def _commit(self, ev, reads, writes):
        for k in writes:
            self.lastw[k] = ev
            self.readers[k] = []
        for k in reads:
            self.readers.setdefault(k, []).append(ev)

    def op(self, eng, fn, reads=(), writes=(), small=False):
        self._sem(eng, 1)
        waits = self._deps(eng, reads, writes, small)
        self.cnt[eng] += 1
        ev = (eng, self.cnt[eng])
        if small:
            self.small_ev.add(ev)
        self.ops[eng].append((waits, fn, eng, 1))
        self._commit(ev, reads, writes)
        return ev

    def finalize(self):
        engs = set(self.ops.keys())
        waited = {e: set() for e in engs}
        for e in engs:
            for waits, fn, semname, inc in self.ops[e]:
                for s_, c_ in waits.items():
                    if s_ in engs:
                        waited[s_].add(c_)
        rank = {}
        for e in engs:
            rank[e] = {c: i + 1 for i, c in enumerate(sorted(waited[e]))}
        out = {e: [] for e in engs}
        for e in engs:
            idx = 0
            for waits, fn, semname, inc in self.ops[e]:
                w2 = {s_: (rank[s_][c_] if s_ in engs else c_) for s_, c_ in waits.items()}
                if semname == e:
                    idx += 1
                    out[e].append((w2, fn, semname, 1 if idx in waited[e] else 0))
                else:
                    out[e].append((w2, fn, semname, inc))
        self.ops = out

    def dma(self, eng, fn, sem, reads=(), writes=()):
        self._sem(sem, 16)
        waits = self._deps(eng, reads, writes)
        self.cnt[sem] += 16
        ev = (sem, self.cnt[sem])
        self.ops[eng].append((waits, fn, sem, 16))
        self._commit(ev, reads, writes)
        return ev

    def wait_all(self, eng, keys):
        waits = self._deps(eng, keys, ())
        self.ops[eng].append((waits, None, None, 0))


def build_program(nt=NT, depth=DEPTH, phases=("lru", "ret", "outp", "ffn")):
    nc = bass.Bass("TRN2", target_bir_lowering=False)
    S = Sched()
    rstage = 4
    rsub = int(os.environ.get("RSUB", "9"))
    for _p in phases:
        if _p.startswith("rstage"):
            rstage = int(_p[6:])

    x_in = nc.dram_tensor("x_in", [NT, 128, KC, T], F32, kind="ExternalInput").ap()
    y_out = nc.dram_tensor("y_out", [NT, 128, KC, T], F32, kind="ExternalOutput").ap()
    WBF = os.environ.get("WBF16", "0") == "1"
    WDT = BF16 if WBF else F32
    w_in = nc.dram_tensor("w_in", [DEPTH, 48, 128, KC, 128], WDT, kind="ExternalInput").ap()
    w_out = nc.dram_tensor("w_out", [DEPTH, 16, 128, KC, 128], WDT, kind="ExternalInput").ap()
    w_gate = nc.dram_tensor("w_gate", [DEPTH, 44, 128, KC, 128], WDT, kind="ExternalInput").ap()
    w_up = nc.dram_tensor("w_up", [DEPTH, 44, 128, KC, 128], WDT, kind="ExternalInput").ap()
    w_down = nc.dram_tensor("w_down", [DEPTH, FQ, 16, 128, FH, 128], WDT, kind="ExternalInput").ap()
    w_gates = nc.dram_tensor("w_gates", [DEPTH, 128, 16, 128], F32, kind="ExternalInput").ap()
    pvec_d = nc.dram_tensor("pvec", [128, PV_TOT], F32, kind="ExternalInput").ap()
    rot_d = nc.dram_tensor("rot", [NT, 128, 2, T], F32, kind="ExternalInput").ap()
    tab_d = nc.dram_tensor("tab", [128, 2, NH, 128], F32, kind="ExternalInput").ap()
    cmat_d = nc.dram_tensor("cmat", [128, 3, 128], F32, kind="ExternalInput").ap()
    csml_d = nc.dram_tensor("csml", [128, 16], F32, kind="ExternalInput").ap()

    import contextlib
    with contextlib.ExitStack() as es:
        def sb(name, shape, dt):
            return es.enter_context(nc.sbuf_tensor(name, shape, dt))

        xT = sb("xT", [128, KC, T], F32)
        bufA = sb("bufA", [128, KC, T], BF16)
        bufB = sb("bufB", [128, KC, T], BF16)
        wbuf = [sb("wbuf%d" % i, [128, KC, 128], BF16) for i in range(NB)]
        wg_sb = sb("wg_sb", [128, 16, 128], BF16)
        pvec = sb("pvec_sb", [128, PV_TOT], F32)
        rot = sb("rot_sb", [128, 2, T], BF16)
        tab = sb("tab_sb", [128, 2, NH, 128], BF16)
        cmat = sb("cmat_sb", [128, 3, 128], BF16)
        csml = sb("csml_sb", [128, 16], F32)
        rstd = sb("rstd", [128, T], F32)
        sqb = [sb("sqb%d" % i, [128, T], BF16) for i in range(2)]
        Sst = sb("Sst", [128, DEPTH, NH, 128], F32)
        hst = sb("hst", [128, DEPTH, 8], F32)
        tail = sb("tail", [128, DEPTH, 8, 3], F32)
        clv = sb("clv", [128, 8], F32)
        spt = [sb("spt%d" % i, [128, 8], F32) for i in range(4)]
        f5 = [sb("f5_%d" % i, [128, 512], F32) for i in range(8)]
        xbp = sb("xbp", [128, T + 3], F32)
        b1 = [sb("b1_%d" % i, [128, T], BF16) for i in range(5)]
        small = sb("small", [128, T], BF16)
        onall = small
        bnst = sb("bnst", [128, 8, 6], F32)
        bnag = sb("bnag", [128, 8, 2], F32)
        grs = sb("grs", [128, 8], F32)
        yacc = rstd

        psA = es.enter_context(nc.psum_tensor("psA", [128, 6, 512], F32))
        psT = es.enter_context(nc.psum_tensor("psT", [128, 2, 1024], BF16))

        sems = {}

        def getsem(name):
            if name not in sems:
                sems[name] = es.enter_context(nc.semaphore("s_" + name))
            return sems[name]

        def pv(col, n=1):
            return pvec[:, col:col + n]

        eps_ap = csml[:, 8:9]
        one_ap = csml[:, 9:10]
        ident = cmat[:, 0, :]
        ones_d = cmat[:, 1, :]
        ones_l = cmat[:, 2, :]

        plan = []
        for tile in range(nt):
            for l in range(depth):
                if "lru" in phases:
                    for c in range(8):
                        plan.append((w_in[l, 32 + c], KC))
                        plan.append((w_in[l, 40 + c], KC))
                if "ret" in phases:
                    for h in range(NH):
                        for base in ((8,), ((8,), (8, 16), (8, 16, 0))[min(rsub, 3) - 1] if rstage == 2 else (8, 16, 0), (8, 16, 0), (8, 16, 0, 24))[rstage - 1]:
                            plan.append((w_in[l, base + h], KC))
                if "outp" in phases:
                    for m in range(16):
                        plan.append((w_out[l, m], KC))
                if "ffn" in phases:
                    for half in range(FQ):
                        for m in range(FH):
                            plan.append((w_gate[l, half * FH + m], KC))
                            plan.append((w_up[l, half * FH + m], KC))
                        for m in range(16):
                            plan.append((w_down[l, half, m], FH))
        wstate = {"issued": 0, "next": 0}

        def w_issue(upto):
            upto = min(upto, len(plan))
            while wstate["issued"] < upto:
                n = wstate["issued"]
                src, kcs = plan[n]
                bi = n % NB
                dst = wbuf[bi][:, 0:kcs, :]
                S.dma("sp" if WBF else "pool", (lambda e, dst=dst, src=src: e.dma_start(out=dst, in_=src)),
                      "w%d" % bi, writes=[("w", bi)])
                wstate["issued"] += 1

        def w_next(kcs):
            n = wstate["next"]
            assert plan[n][1] == kcs
            w_issue(n + NB)
            wstate["next"] += 1
            bi = n % NB
            return wbuf[bi], ("wr", bi)

        pstate = {"bank": 0, "tslot": 0}

        def nextbank():
            b = pstate["bank"]
            pstate["bank"] = (b + 1) % 6
            return b

        def proj(rhs_buf, rhs_key, kcs, evac):
            wb, wkey = w_next(kcs)
            banks = (nextbank(), nextbank())

            def mm(e, wb=wb, banks=banks):
                ins = None
                for kc in range(kcs):
                    for th in range(2):
                        ins = e.matmul(psA[:, banks[th], :], lhsT=wb[:, kc, :],
                                       rhs=rhs_buf[:, kc, th * 512:(th + 1) * 512],
                                       start=(kc == 0), stop=(kc == kcs - 1))
                return ins
            S.op("pe", mm, reads=[wkey, rhs_key], writes=[("ps", banks[0]), ("ps", banks[1])])
            for th in range(2):
                evac(th, psA[:, banks[th], :], ("ps", banks[th]))

        def mslot():
            b = nextbank()
            return psA[:, b, 0:128], ("ps", b)

        def tslot():
            s = pstate["tslot"]
            pstate["tslot"] = (s + 1) % 2
            return psT[:, s, 0:512], ("ps", 6 + s)

        def rmsnorm(gcol, out_fn):
            nb = (nextbank(), nextbank())
            for kc in range(KC):
                sq = sqb[kc % 2]
                S.op("act", (lambda e, sq=sq, kc=kc: e.activation(out=sq[:, :], in_=xT[:, kc, :], func=AF.Square)),
                     reads=[("x", kc)], writes=[("sqb", kc % 2)])

                def mm(e, sq=sq, kc=kc):
                    ins = None
                    for th in range(2):
                        ins = e.matmul(psA[:, nb[th], :], lhsT=ones_d, rhs=sq[:, th * 512:(th + 1) * 512],
                                       start=(kc == 0), stop=(kc == KC - 1))
                    return ins
                S.op("pe", mm, reads=[("sqb", kc % 2), "cmat"], writes=[("ps", nb[0]), ("ps", nb[1])])
            for th in range(2):
                S.op("act", (lambda e, th=th: e.activation(out=rstd[:, th * 512:(th + 1) * 512], in_=psA[:, nb[th], :],
                                                           func=AF.Sqrt, bias=eps_ap, scale=1.0)),
                     reads=[("ps", nb[th]), "csml"], writes=[("rstd", th)])
                S.op("dve", (lambda e, th=th: e.reciprocal(out=rstd[:, th * 512:(th + 1) * 512],
                                                           in_=rstd[:, th * 512:(th + 1) * 512])),
                     reads=[("rstd", th)], writes=[("rstd", th)])
            for kc in range(KC):
                out_fn(kc, gcol + kc)

        def norm_to_bufA(kc, gc):
            S.op("dve", (lambda e, kc=kc, gc=gc: e.scalar_tensor_tensor(
                out=bufA[:, kc, :], in0=xT[:, kc, :], scalar=pv(gc), in1=rstd[:, :],
                op0=ALU.mult, op1=ALU.mult)),
                reads=[("x", kc), ("rstd", 0), ("rstd", 1), "pvec"], writes=[("A", kc)])

        def norm_inplace(kc, gc):
            S.op("dve", (lambda e, kc=kc, gc=gc: e.scalar_tensor_tensor(
                out=xT[:, kc, :], in0=xT[:, kc, :], scalar=pv(gc), in1=rstd[:, :],
                op0=ALU.mult, op1=ALU.mult)),
                reads=[("x", kc), ("rstd", 0), ("rstd", 1), "pvec"], writes=[("x", kc)])

        A_keys = [("A", kc) for kc in range(KC)]

        class AKey:
            pass

        S.dma("pool", lambda e: e.dma_start(out=pvec[:, :], in_=pvec_d[:, :]), "ld0", writes=["pvec"])
        S.dma("pool", lambda e: e.dma_start(out=tab[:, :, :, :], in_=tab_d[:, :, :, :]), "ld1", writes=["tab"])
        S.dma("pool", lambda e: e.dma_start(out=cmat[:, :, :], in_=cmat_d[:, :, :]), "ld2", writes=["cmat"])
        S.dma("pool", lambda e: e.dma_start(out=csml[:, :], in_=csml_d[:, :]), "ld3", writes=["csml"])
        S.op("dve", lambda e: e.memset(Sst[:, :, :, :], 0.0), writes=["Sst"])
        S.op("dve", lambda e: e.memset(hst[:, :, :], 0.0), writes=["hst"])
        S.op("dve", lambda e: e.memset(tail[:, :, :, :], 0.0), writes=["tail"])

        for tile in range(nt):
            S.dma("sp", (lambda e, tile=tile: e.dma_start(out=xT[:, :, :], in_=x_in[tile])), "ldx",
                  writes=[("x", kc) for kc in range(KC)])
            S.dma("pool", (lambda e, tile=tile: e.dma_start(out=rot[:, :, :], in_=rot_d[tile])), "ldr",
                  writes=["rot"])
            for l in range(depth):
                pb = PV_L * l
                S.dma("pool", (lambda e, l=l: e.dma_start(out=wg_sb[:, :, :], in_=w_gates[l])), "ldg",
                      writes=["wg"])
                lam = pv(pb + PV_LAM, 8)
                s0, s1, s2, s3 = spt
                S.op("dve", lambda e, lam=lam: e.tensor_scalar(out=s0[:, :], in0=lam, scalar1=-1.0, scalar2=None, op0=ALU.mult),
                     reads=["pvec"], writes=["s0"], small=True)
                S.op("dve", lambda e, lam=lam: e.tensor_tensor(out=s0[:, :], in0=s0[:, :], in1=lam, op=ALU.max),
                     reads=["pvec", "s0"], writes=["s0"], small=True)
                S.op("act", lambda e: e.activation(out=s0[:, :], in_=s0[:, :], func=AF.Exp, scale=-1.0),
                     reads=["s0"], writes=["s0"], small=True)
                S.op("dve", lambda e: e.tensor_scalar(out=s1[:, :], in0=s0[:, :], scalar1=2.0, scalar2=None, op0=ALU.add),
                     reads=["s0"], writes=["s1"], small=True)
                S.op("dve", lambda e: e.reciprocal(out=s1[:, :], in_=s1[:, :]), reads=["s1"], writes=["s1"], small=True)
                S.op("dve", lambda e: e.tensor_tensor(out=s1[:, :], in0=s1[:, :], in1=s0[:, :], op=ALU.mult),
                     reads=["s1", "s0"], writes=["s1"], small=True)
                S.op("dve", lambda e: e.tensor_tensor(out=s2[:, :], in0=s1[:, :], in1=s1[:, :], op=ALU.mult),
                     reads=["s1"], writes=["s2"], small=True)
                S.op("dve", lambda e: e.tensor_scalar(out=s3[:, :], in0=s2[:, :], scalar1=1.0 / 13.0, scalar2=1.0 / 11.0,
                                                      op0=ALU.mult, op1=ALU.add), reads=["s2"], writes=["s3"], small=True)
                for kk in (9.0, 7.0, 5.0, 3.0, 1.0):
                    S.op("dve", lambda e: e.tensor_tensor(out=s3[:, :], in0=s3[:, :], in1=s2[:, :], op=ALU.mult),
                         reads=["s3", "s2"], writes=["s3"], small=True)
                    S.op("dve", lambda e, kk=kk: e.tensor_scalar(out=s3[:, :], in0=s3[:, :], scalar1=1.0 / kk, scalar2=None,
                                                                 op0=ALU.add), reads=["s3"], writes=["s3"], small=True)
                S.op("dve", lambda e: e.tensor_tensor(out=s3[:, :], in0=s3[:, :], in1=s1[:, :], op=ALU.mult),
                     reads=["s3", "s1"], writes=["s3"], small=True)
                S.op("dve", lambda e, lam=lam: e.tensor_scalar(out=s0[:, :], in0=lam, scalar1=-1.0, scalar2=0.0,
                                                               op0=ALU.mult, op1=ALU.max), reads=["pvec", "s0"], writes=["s0"], small=True)
                S.op("dve", lambda e: e.scalar_tensor_tensor(out=s0[:, :], in0=s3[:, :], scalar=2.0, in1=s0[:, :],
                                                             op0=ALU.mult, op1=ALU.add), reads=["s3", "s0"], writes=["s0"], small=True)
                S.op("dve", lambda e: e.tensor_scalar(out=clv[:, :], in0=s0[:, :], scalar1=-8.0, scalar2=None, op0=ALU.mult),
                     reads=["s0"], writes=["clv"], small=True)

                rmsnorm(pb + PV_N1G, norm_to_bufA)

                if "lru" not in phases:
                    S.op("dve", lambda e: e.memset(bufB[:, 8:16, :], 0.0), writes=[("B", m_, t_) for m_ in range(8, 16) for t_ in range(2)])
                for c in (range(8) if "lru" in phases else ()):
                    S.op("act", (lambda e, c=c, l=l: e.activation(out=xbp[:, 0:3], in_=tail[:, l, c, :], func=AF.Copy)),
                         reads=["tail"], writes=[("xbp", -1)], small=True)

                    def ev_xb(th, ps, pk, c=c):
                        S.op("act", (lambda e, th=th, ps=ps: e.activation(out=xbp[:, 3 + th * 512:3 + (th + 1) * 512],
                                                                          in_=ps, func=AF.Copy)),
                             reads=[pk], writes=[("xbp", th)])
                    proj(bufA, "Aall", KC, ev_xb)
                    S.op("act", (lambda e, c=c, l=l: e.activation(out=tail[:, l, c, :], in_=xbp[:, T:T + 3], func=AF.Copy)),
                         reads=[("xbp", 1)], writes=["tail"], small=True)

                    steps = {0: [], 1: []}

                    def ev_yb(th, ps, pk, c=c, l=l, pb=pb, steps=steps):
                        def emit(*a_, **k_):
                            steps[th].append((a_, k_))
                        gb = b1[2 + th]
                        gk = ("b1", 2 + th)
                        xc, t1, t2, t3 = f5[4 * th + 0], f5[4 * th + 1], f5[4 * th + 2], f5[4 * th + 3]
                        kx, k1, k2, k3 = [("f5", 4 * th + i) for i in range(4)]
                        xcb = b1[th]
                        o0 = th * 512
                        emit("act", (lambda e: e.activation(out=t3[:, :], in_=ps, func=AF.Square)),
                             reads=[pk, k3], writes=[k3])
                        emit("dve", (lambda e: e.tensor_scalar(out=t3[:, :], in0=t3[:, :], scalar1=0.044715, scalar2=1.0,
                                                               op0=ALU.mult, op1=ALU.add)), reads=[k3], writes=[k3])
                        emit("dve", (lambda e: e.tensor_tensor(out=t3[:, :], in0=ps, in1=t3[:, :], op=ALU.mult)),
                             reads=[pk, k3], writes=[k3])
                        emit("act", (lambda e: e.activation(out=t3[:, :], in_=t3[:, :], func=AF.Sigmoid, scale=1.5957691216057308)),
                             reads=[k3], writes=[k3])
                        emit("dve", (lambda e: e.tensor_tensor(out=gb[:, 0:512], in0=ps, in1=t3[:, :], op=ALU.mult)),
                             reads=[pk, k3], writes=[gk])
                        cw = pb + PV_CW
                        xdeps = [("xbp", -1), ("xbp", 0), ("xbp", 1)]
                        emit("act", (lambda e: e.activation(out=xc[:, :], in_=xbp[:, o0 + 3:o0 + 515], func=AF.Identity,
                                                            bias=pv(pb + PV_CB + c), scale=pv(cw + 3 * 8 + c))),
                             reads=xdeps + ["pvec"], writes=[kx])
                        for j in range(3):
                            emit("dve", (lambda e, j=j: e.scalar_tensor_tensor(
                                out=xc[:, :], in0=xbp[:, o0 + j:o0 + j + 512], scalar=pv(cw + j * 8 + c), in1=xc[:, :],
                                op0=ALU.mult, op1=ALU.add)), reads=xdeps + [kx, "pvec"], writes=[kx])
                        emit("act", (lambda e: e.activation(out=xcb[:, 0:512], in_=xc[:, :], func=AF.Copy)),
                             reads=[kx], writes=[("b1", th)])
                        rb_, ib_ = nextbank(), nextbank()
                        rps = psA[:, rb_, :]
                        ips = psA[:, ib_, :]
                        rk, ik = ("ps", rb_), ("ps", ib_)
                        emit("pe", (lambda e: e.matmul(rps, lhsT=wg_sb[:, c, :], rhs=xcb[:, 0:512], start=True, stop=True)),
                             reads=[("b1", th), "wg"], writes=[rk])
                        emit("pe", (lambda e: e.matmul(ips, lhsT=wg_sb[:, 8 + c, :], rhs=xcb[:, 0:512], start=True, stop=True)),
                             reads=[("b1", th), "wg"], writes=[ik])
                        emit("act", (lambda e: e.activation(out=t1[:, :], in_=rps, func=AF.Sigmoid, bias=pv(pb + PV_BA + c))),
                             reads=[rk, "pvec"], writes=[k1])
                        emit("act", (lambda e: e.activation(out=t1[:, :], in_=t1[:, :], func=AF.Exp, scale=clv[:, c:c + 1])),
                             reads=[k1, "clv"], writes=[k1])
                        emit("act", (lambda e: e.activation(out=t3[:, :], in_=ips, func=AF.Sigmoid, bias=pv(pb + PV_BX + c))),
                             reads=[ik, "pvec"], writes=[k3])
                        emit("dve", (lambda e: e.tensor_tensor(out=t2[:, :], in0=t1[:, :], in1=t1[:, :], op=ALU.mult)),
                             reads=[k1], writes=[k2])
                        emit("act", (lambda e: e.activation(out=t2[:, :], in_=t2[:, :], func=AF.Sqrt, bias=one_ap, scale=-1.0)),
                             reads=[k2, "csml"], writes=[k2])
                        emit("dve", (lambda e: e.tensor_tensor(out=t3[:, :], in0=t3[:, :], in1=xc[:, :], op=ALU.mult)),
                             reads=[k3, kx], writes=[k3])
                        emit("dve", (lambda e: e.tensor_tensor(out=t3[:, :], in0=t3[:, :], in1=t2[:, :], op=ALU.mult)),
                             reads=[k3, k2], writes=[k3])
                        emit("dve", (lambda e: e.tensor_tensor_scan(out=t2[:, :], data0=t1[:, :], data1=t3[:, :],
                                                                    initial=hst[:, l, c:c + 1], op0=ALU.mult, op1=ALU.add)),
                             reads=[k1, k3, "hst", k2], writes=[k2])
                        emit("dve", (lambda e: e.tensor_copy(out=hst[:, l, c:c + 1], in_=t2[:, 511:512])),
                             reads=[k2], writes=["hst"], small=True)
                        emit("dve", (lambda e: e.tensor_tensor(out=t2[:, :], in0=t2[:, :], in1=gb[:, 0:512], op=ALU.mult)),
                             reads=[k2, gk], writes=[k2])
                        emit("act", (lambda e: e.activation(out=t3[:, :], in_=t2[:, :], func=AF.Square)),
                             reads=[k2], writes=[k3])
                        if c == 0:
                            emit("dve", (lambda e: e.tensor_copy(out=yacc[:, o0:o0 + 512], in_=t3[:, :])),
                                 reads=[k3], writes=[("rstd", th)])
                        else:
                            emit("dve", (lambda e: e.tensor_tensor(out=yacc[:, o0:o0 + 512], in0=yacc[:, o0:o0 + 512],
                                                                   in1=t3[:, :], op=ALU.add)),
                                 reads=[k3, ("rstd", th)], writes=[("rstd", th)])
                        emit("act", (lambda e: e.activation(out=bufB[:, 8 + c, o0:o0 + 512], in_=t2[:, :], func=AF.Identity,
                                                            scale=pv(pb + PV_LNG + c))),
                             reads=[k2, "pvec"], writes=[("B", 8 + c, th)])
                    proj(bufA, "Aall", KC, ev_yb)
                    LAG = 3
                    for i_ in range(max(len(steps[0]), len(steps[1]) + LAG)):
                        if i_ < len(steps[0]):
                            S.op(*steps[0][i_][0], **steps[0][i_][1])
                        if 0 <= i_ - LAG < len(steps[1]):
                            S.op(*steps[1][i_ - LAG][0], **steps[1][i_ - LAG][1])
                for th in (range(2) if "lru" in phases else ()):
                    o0 = th * 512
                    lb_ = nextbank()
                    S.op("act", (lambda e, th=th, o0=o0: e.activation(out=sqb[0][:, o0:o0 + 512], in_=yacc[:, o0:o0 + 512], func=AF.Copy)),
                         reads=[("rstd", th)], writes=[("sqb", 0)])
                    S.op("pe", (lambda e, th=th, o0=o0, lb_=lb_: e.matmul(psA[:, lb_, :], lhsT=ones_l, rhs=sqb[0][:, o0:o0 + 512],
                                                                           start=True, stop=True)),
                         reads=[("sqb", 0), "cmat"], writes=[("ps", lb_)])
                    S.op("act", (lambda e, th=th, o0=o0, lb_=lb_: e.activation(out=rstd[:, o0:o0 + 512], in_=psA[:, lb_, :],
                                                                               func=AF.Sqrt, bias=eps_ap, scale=1.0)),
                         reads=[("ps", lb_), "csml"], writes=[("rstd", th)])
                    S.op("dve", (lambda e, o0=o0: e.reciprocal(out=rstd[:, o0:o0 + 512], in_=rstd[:, o0:o0 + 512])),
                         reads=[("rstd", th)], writes=[("rstd", th)])
                for c in (range(8) if "lru" in phases else ()):
                    S.op("dve", (lambda e, c=c: e.tensor_tensor(out=bufB[:, 8 + c, :], in0=bufB[:, 8 + c, :], in1=rstd[:, :],
                                                                op=ALU.mult)),
                         reads=[("B", 8 + c, 0), ("B", 8 + c, 1), ("rstd", 0), ("rstd", 1)],
                         writes=[("B", 8 + c, 0), ("B", 8 + c, 1)])

                kT, vT, qT, ktok, vtok = b1
                qd = vT
                gsil = sqb[1]
                rt1 = [f5[0], f5[4]]
                rt2 = [f5[1], f5[5]]
                rk1 = [("f5", 0), ("f5", 4)]
                rk2 = [("f5", 1), ("f5", 5)]

                def rotary_evac(dst, dkey):
                    def ev(th, ps, pk):
                        o0 = th * 512
                        t1, t2, k1, k2 = rt1[th], rt2[th], rk1[th], rk2[th]
                        S.op("dve", (lambda e: e.tensor_tensor(out=t1[:, :], in0=ps, in1=rot[:, 0, o0:o0 + 512], op=ALU.mult)),
                             reads=[pk, "rot"], writes=[k1])
                        S.op("dve", (lambda e: e.tensor_tensor(out=t2[0:64, :], in0=ps[64:128, :], in1=rot[0:64, 1, o0:o0 + 512],
                                                               op=ALU.mult)), reads=[pk, "rot"], writes=[k2])
                        S.op("dve", (lambda e: e.tensor_tensor(out=t2[64:128, :], in0=ps[0:64, :], in1=rot[64:128, 1, o0:o0 + 512],
                                                               op=ALU.mult)), reads=[pk, "rot", k2], writes=[k2])
                        S.op("dve", (lambda e: e.tensor_tensor(out=dst[:, o0:o0 + 512], in0=t1[:, :], in1=t2[:, :], op=ALU.add)),
                             reads=[k1, k2], writes=[("b1", dkey, th)])
                    return ev

                if "ret" not in phases:
                    S.op("dve", lambda e: e.memset(bufB[:, 0:8, :], 0.0), writes=[("B", m_, t_) for m_ in range(8) for t_ in range(2)])
                for h in (range(NH) if "ret" in phases else ()):
                    gam = 1.0 - 2.0 ** (-5.0 - h)
                    proj(bufA, "Aall", KC, rotary_evac(kT, 0))
                    if rstage < 2:
                        S.op("dve", lambda e, h=h: e.memset(bufB[:, h, :], 0.0), writes=[("B", h, 0), ("B", h, 1)])
                        continue
                    for g4 in range(2):
                        tp, tk = tslot()

                        def trk(e, g4=g4, tp=tp):
                            ins = None
                            for jj in range(4):
                                j = g4 * 4 + jj
                                ins = e.transpose(tp[:, jj * 128:(jj + 1) * 128], kT[:, j * 128:(j + 1) * 128], ident)
                            return ins
                        S.op("pe", trk, reads=[("b1", 0, g4), "cmat"], writes=[tk])
                        S.op("act", (lambda e, g4=g4, tp=tp, h=h: e.activation(out=ktok[:, g4 * 512:(g4 + 1) * 512], in_=tp,
                                                                               func=AF.Identity, scale=csml[:, h:h + 1])),
                             reads=[tk, "csml"], writes=[("b1", 3, g4)])

                    if rstage == 2 and rsub < 2:
                        S.op("dve", lambda e, h=h: e.memset(bufB[:, h, :], 0.0), writes=[("B", h, 0), ("B", h, 1)])
                        continue
                    def ev_v(th, ps, pk):
                        S.op("act", (lambda e: e.activation(out=vT[:, th * 512:(th + 1) * 512], in_=ps, func=AF.Copy)),
                             reads=[pk], writes=[("b1", 1, th)])
                    proj(bufA, "Aall", KC, ev_v)
                    for g4 in range(2):
                        tp, tk = tslot()

                        def trv(e, g4=g4, tp=tp):
                            ins = None
                            for jj in range(4):
                                j = g4 * 4 + jj
                                ins = e.transpose(tp[:, jj * 128:(jj + 1) * 128], vT[:, j * 128:(j + 1) * 128], ident)
                            return ins
                        S.op("pe", trv, reads=[("b1", 1, g4), "cmat"], writes=[tk])
                        S.op("act", (lambda e, g4=g4, tp=tp: e.activation(out=vtok[:, g4 * 512:(g4 + 1) * 512], in_=tp, func=AF.Copy)),
                             reads=[tk], writes=[("b1", 4, g4)])
                    if rstage == 2 and rsub < 3:
                        S.op("dve", lambda e, h=h: e.memset(bufB[:, h, :], 0.0), writes=[("B", h, 0), ("B", h, 1)])
                        continue
                    proj(bufA, "Aall", KC, rotary_evac(qT, 2))
                    for j in range(8):
                        S.op("dve", (lambda e, j=j, h=h: e.tensor_tensor(out=qd[:, j * 128:(j + 1) * 128],
                                                                         in0=qT[:, j * 128:(j + 1) * 128],
                                                                         in1=tab[:, 1, h, :], op=ALU.mult)),
                             reads=[("b1", 2, j // 4), "tab"], writes=[("b1", 1, j // 4)])
                    if rstage < 3:
                        S.op("dve", lambda e, h=h: e.memset(bufB[:, h, :], 0.0), writes=[("B", h, 0), ("B", h, 1)])
                        continue
                    g128 = float(gam ** 128)
                    ub = (nextbank(), nextbank())

                    def mmu(e, ub=ub):
                        ins = None
                        for j in range(8):
                            js = slice(j * 128, (j + 1) * 128)
                            ins = e.matmul(psA[:, ub[j // 4], (j % 4) * 128:(j % 4 + 1) * 128], lhsT=ktok[:, js], rhs=vtok[:, js],
                                           start=True, stop=True)
                        return ins
                    S.op("pe", mmu, reads=[("b1", 3, 0), ("b1", 3, 1), ("b1", 4, 0), ("b1", 4, 1)],
                         writes=[("ps", ub[0]), ("ps", ub[1])])
                    if rstage == 3 and rsub < 2:
                        S.op("dve", lambda e, h=h: e.memset(bufB[:, h, :], 0.0), writes=[("B", h, 0), ("B", h, 1)])
                        continue
                    sbk = (nextbank(), nextbank())

                    def mms(e, sbk=sbk):
                        ins = None
                        for j in range(8):
                            js = slice(j * 128, (j + 1) * 128)
                            ins = e.matmul(psA[:, sbk[j // 4], (j % 4) * 128:(j % 4 + 1) * 128], lhsT=kT[:, js], rhs=qT[:, js],
                                           start=True, stop=True)
                        return ins
                    S.op("pe", mms, reads=[("b1", 0, 0), ("b1", 0, 1), ("b1", 2, 0), ("b1", 2, 1)],
                         writes=[("ps", sbk[0]), ("ps", sbk[1])])
                    for j in range(8):
                        S.op("dve", (lambda e, j=j, h=h, sbk=sbk: e.tensor_tensor(
                            out=small[:, j * 128:(j + 1) * 128], in0=psA[:, sbk[j // 4], (j % 4) * 128:(j % 4 + 1) * 128],
                            in1=tab[:, 0, h, :], op=ALU.mult)),
                            reads=[("ps", sbk[j // 4]), "tab"], writes=[("small", j // 4)])
                    if rstage == 3 and rsub < 3:
                        S.op("dve", lambda e, h=h: e.memset(bufB[:, h, :], 0.0), writes=[("B", h, 0), ("B", h, 1)])
                        continue
                    def sslot(j):
                        return f5[6 + j // 4][:, (j % 4) * 128:(j % 4 + 1) * 128]
                    S.op("act", (lambda e, h=h, l=l: e.activation(out=sslot(0), in_=Sst[:, l, h, :], func=AF.Copy)),
                         reads=["Sst"], writes=[("f5", 6)])
                    for j in range(8):
                        dst = sslot(j + 1) if j < 7 else Sst[:, l, h, :]
                        S.op("dve", (lambda e, j=j, dst=dst, ub=ub, g128=g128: e.scalar_tensor_tensor(
                            out=dst, in0=sslot(j), scalar=g128,
                            in1=psA[:, ub[j // 4], (j % 4) * 128:(j % 4 + 1) * 128],
                            op0=ALU.mult, op1=ALU.add)),
                            reads=[("ps", ub[j // 4]), ("f5", 6 + j // 4)],
                            writes=[("f5", 6 + (j + 1) // 4)] if j < 7 else ["Sst"], small=True)
                    for g4 in range(2):
                        S.op("act", (lambda e, g4=g4: e.activation(out=kT[:, g4 * 512:(g4 + 1) * 512], in_=f5[6 + g4][:, :], func=AF.Copy)),
                             reads=[("f5", 6 + g4)], writes=[("b1", 0, g4)])
                    if rstage == 3 and rsub < 4:
                        S.op("dve", lambda e, h=h: e.memset(bufB[:, h, :], 0.0), writes=[("B", h, 0), ("B", h, 1)])
                        continue
                    ob1 = (nextbank(), nextbank())

                    def mmo1(e, ob1=ob1):
                        ins = None
                        for j in range(8):
                            js = slice(j * 128, (j + 1) * 128)
                            ins = e.matmul(psA[:, ob1[j // 4], (j % 4) * 128:(j % 4 + 1) * 128], lhsT=small[:, js], rhs=vtok[:, js],
                                           start=True, stop=True)
                        return ins
                    S.op("pe", mmo1, reads=[("small", 0), ("small", 1), ("b1", 4, 0), ("b1", 4, 1)],
                         writes=[("ps", ob1[0]), ("ps", ob1[1])])
                    ob2 = (nextbank(), nextbank())

                    def mmo2(e, ob2=ob2):
                        ins = None
                        for j in range(8):
                            js = slice(j * 128, (j + 1) * 128)
                            ins = e.matmul(psA[:, ob2[j // 4], (j % 4) * 128:(j % 4 + 1) * 128], lhsT=qd[:, js], rhs=kT[:, js],
                                           start=True, stop=True)
                        return ins
                    S.op("pe", mmo2, reads=[("b1", 1, 0), ("b1", 1, 1), ("b1", 0, 0), ("b1", 0, 1)],
                         writes=[("ps", ob2[0]), ("ps", ob2[1])])
                    for g4 in range(2):
                        S.op("act", (lambda e, g4=g4, ob1=ob1: e.activation(out=f5[2 + g4][:, :], in_=psA[:, ob1[g4], :], func=AF.Copy)),
                             reads=[("ps", ob1[g4])], writes=[("f5", 2 + g4)])
                        S.op("dve", (lambda e, g4=g4, ob2=ob2: e.tensor_tensor(out=f5[2 + g4][:, :], in0=psA[:, ob2[g4], :],
                                                                               in1=f5[2 + g4][:, :], op=ALU.add)),
                             reads=[("ps", ob2[g4]), ("f5", 2 + g4)], writes=[("f5", 2 + g4)])
                    if rstage == 3 and rsub < 5:
                        S.op("dve", lambda e, h=h: e.memset(bufB[:, h, :], 0.0), writes=[("B", h, 0), ("B", h, 1)])
                        continue
                    for j in range(8):
                        S.op("dve", (lambda e, j=j: e.bn_stats(out=bnst[:, j, :],
                                                               in_=f5[2 + j // 4][:, (j % 4) * 128:(j % 4 + 1) * 128])),
                             reads=[("f5", 2 + j // 4)], writes=[("bnst", j)])
                    for j in range(8):
                        S.op("dve", (lambda e, j=j: e.bn_aggr(out=bnag[:, j, :], in_=bnst[:, j, :])),
                             reads=[("bnst", j)], writes=[("bnag", j)], small=True)
                    if rstage < 4:
                        S.op("dve", lambda e, h=h: e.memset(bufB[:, h, :], 0.0), writes=[("B", h, 0), ("B", h, 1)])
                        continue
                    def ev_g(th, ps, pk):
                        tg, tgk = f5[6 + th], ("f5", 6 + th)
                        S.op("act", (lambda e: e.activation(out=tg[:, :], in_=ps, func=AF.Sigmoid)),
                             reads=[pk], writes=[tgk])
                        S.op("dve", (lambda e: e.tensor_tensor(out=gsil[:, th * 512:(th + 1) * 512], in0=ps, in1=tg[:, :], op=ALU.mult)),
                             reads=[pk, tgk], writes=[("sqb", 1, th)])
                    proj(bufA, "Aall", KC, ev_g)
                    bkeys = [("bnag", j) for j in range(8)]
                    S.op("act", (lambda e: e.activation(out=grs[:, :], in_=bnag[:, :, 1], func=AF.Sqrt, bias=eps_ap, scale=1.0)),
                         reads=bkeys + ["csml"], writes=["grs"], small=True)
                    S.op("dve", (lambda e: e.reciprocal(out=grs[:, :], in_=grs[:, :])), reads=["grs"], writes=["grs"], small=True)
                    for j in range(8):
                        S.op("dve", (lambda e, j=j: e.tensor_scalar(out=onall[:, j * 128:(j + 1) * 128],
                                                                    in0=f5[2 + j // 4][:, (j % 4) * 128:(j % 4 + 1) * 128],
                                                                    scalar1=bnag[:, j, 0:1], scalar2=grs[:, j:j + 1],
                                                                    op0=ALU.subtract, op1=ALU.mult)),
                             reads=[("f5", 2 + j // 4), ("bnag", j), "grs"], writes=[("small", j // 4)])
                    for g4 in range(2):
                        tp, tk = tslot()

                        def trn(e, g4=g4, tp=tp):
                            ins = None
                            for jj in range(4):
                                j = g4 * 4 + jj
                                ins = e.transpose(tp[:, jj * 128:(jj + 1) * 128], onall[:, j * 128:(j + 1) * 128], ident)
                            return ins
                        S.op("pe", trn, reads=[("small", g4), "cmat"], writes=[tk])
                        S.op("dve", (lambda e, g4=g4, tp=tp, h=h, pb=pb: e.scalar_tensor_tensor(
                            out=bufB[:, h, g4 * 512:(g4 + 1) * 512], in0=tp, scalar=pv(pb + PV_GNG + h),
                            in1=gsil[:, g4 * 512:(g4 + 1) * 512], op0=ALU.mult, op1=ALU.mult)),
                            reads=[tk, ("sqb", 1, g4), "pvec"], writes=[("B", h, g4)])

                def ev_res(mo):
                    def ev(th, ps, pk):
                        o0 = th * 512
                        S.op("dve", (lambda e: e.tensor_tensor(out=xT[:, mo, o0:o0 + 512], in0=ps, in1=xT[:, mo, o0:o0 + 512],
                                                               op=ALU.add)), reads=[pk, ("x", mo)], writes=[("x", mo)])
                    return ev
                for mo in (range(16) if "outp" in phases else ()):
                    proj(bufB, "Ball", KC, ev_res(mo))

                if "ffn" in phases:
                    rmsnorm(pb + PV_N2G, norm_to_bufA)
                for half in (range(FQ) if "ffn" in phases else ()):
                    for m in range(FH):
                        def ev_gate(th, ps, pk):
                            S.op("act", (lambda e: e.activation(out=f5[2 + 4 * th][:, :], in_=ps, func=AF.Sigmoid)),
                                 reads=[pk], writes=[("f5", 2 + 4 * th)])
                            S.op("dve", (lambda e: e.tensor_tensor(out=f5[2 + 4 * th][:, :], in0=ps, in1=f5[2 + 4 * th][:, :], op=ALU.mult)),
                                 reads=[pk, ("f5", 2 + 4 * th)], writes=[("f5", 2 + 4 * th)])
                        proj(bufA, "Aall", KC, ev_gate)

                        def ev_up(th, ps, pk, m=m):
                            o0 = th * 512
                            S.op("dve", (lambda e: e.tensor_tensor(out=bufB[:, m, o0:o0 + 512], in0=ps, in1=f5[2 + 4 * th][:, :],
                                                                   op=ALU.mult)), reads=[pk, ("f5", 2 + 4 * th)], writes=[("B", m, th)])
                        proj(bufA, "Aall", KC, ev_up)
                    for mo in range(16):
                        proj(bufB, "Ball", FH, ev_res(mo))

            rmsnorm(PV_FINAL, norm_inplace)
            S.dma("sp", (lambda e, tile=tile: e.dma_start(out=y_out[tile], in_=xT[:, :, :])), "st",
                  reads=[("x", kc) for kc in range(KC)])
        S.wait_all("sp", [])
        final_st = S.cnt["st"]

        engmap = {"pe": "tensor", "act": "scalar", "dve": "vector", "pool": "gpsimd", "sp": "sync"}
        if os.environ.get("DENSE", "0") != "1":
            S.finalize()
        for _n in list(S.cnt.keys()):
            getsem(_n)
        with nc.Block() as block:
            def make(engname):
                def body(e):
                    for waits, fn, semname, inc in S.ops[engname]:
                        for sname, val in waits.items():
                            e.wait_ge(getsem(sname), val)
                        if fn is None:
                            continue
                        ins = fn(e)
                        if inc:
                            ins.then_inc(getsem(semname), inc)
                    if engname == "sp":
                        e.wait_ge(getsem("st"), final_st)
                return body
            for engname, attr in engmap.items():
                getattr(block, attr)(make(engname))
    return nc


_orig_deps = Sched._deps
_orig_commit = Sched._commit


def _expand(keys):
    out = []
    for k in keys:
        if k == "Aall":
            out.extend(("A", kc) for kc in range(KC))
        elif isinstance(k, tuple) and k[0] == "wr":
            out.append(("w", k[1]))
        elif isinstance(k, tuple) and k[0] == "pmb":
            out.extend(("pm", (k[1] - 4) * 4 + i) for i in range(4))
        elif isinstance(k, tuple) and k[0] == "sqb" and len(k) == 2:
            out.extend([("sqb", k[1], 0), ("sqb", k[1], 1)])
        elif isinstance(k, tuple) and k[0] == "b1" and len(k) == 2:
            out.extend([("b1", k[1], 0), ("b1", k[1], 1)])
        elif k == "Ball":
            out.extend(("B", m, th) for m in range(KC) for th in range(2))
        else:
            out.append(k)
    return out


def _deps2(self, eng, reads, writes, small=False):
    return _orig_deps(self, eng, _expand(reads), _expand(writes), small)


def _commit2(self, ev, reads, writes):
    return _orig_commit(self, ev, _expand(reads), _expand(writes))


Sched._deps = _deps2
Sched._commit = _commit2


def _tile_w(w, kcs):
    K, N = w.shape
    assert K == kcs * 128
    return np.ascontiguousarray(w.reshape(kcs, 128, N // 128, 128).transpose(2, 1, 0, 3))


def _host_tables():
    f32 = np.float32
    H, C = NH, CH
    log_g = np.log1p(-np.exp2(-5.0 - np.arange(H, dtype=np.float64)))
    idx = np.arange(128, dtype=np.float64)
    a = idx[None, :]
    c = idx[:, None]
    same_or_prev = (np.floor(c / C) <= np.floor(a / C))
    mask = np.exp(log_g[:, None, None] * np.abs(a - c)[None]) * same_or_prev[None] * (DK ** -0.5)
    qdec = np.exp(log_g[:, None] * (idx + 1.0)[None, :])
    kdec = np.exp(log_g[:, None] * (127.0 - idx)[None, :]) * (DK ** -0.5)
    tab = np.zeros((128, 2, H, 128), f32)
    tab[:, 0] = mask.transpose(1, 0, 2)
    tab[:, 1] = np.broadcast_to(qdec[None], (128, H, 128))
    cmat = np.zeros((128, 3, 128), f32)
    cmat[:, 0] = np.eye(128)
    cmat[:, 1] = 1.0 / D
    cmat[:, 2] = 1.0 / 1024.0
    csml = np.zeros((128, 16), f32)
    csml[:, 0:8] = kdec.T
    csml[:, 8] = EPS
    csml[:, 9] = 1.0
    inv = 1.0 / (10000.0 ** (np.arange(0, DK, 2, dtype=np.float64) / DK))
    rot = np.zeros((NT, 128, 2, T), f32)
    for t in range(NT):
        pos = np.arange(t * T, (t + 1) * T, dtype=np.float64)
        ang = (pos[None, :].astype(np.float32) * inv[:, None].astype(np.float32)).astype(np.float32)
        cs, sn = np.cos(ang.astype(np.float64)), np.sin(ang.astype(np.float64))
        rot[t, 0:64, 0] = cs
        rot[t, 64:128, 0] = cs
        rot[t, 0:64, 1] = -sn
        rot[t, 64:128, 1] = sn
    return tab, cmat, csml, rot


def prep(x, norm1_g, w_in, ret_gn_g, lru_conv_w, lru_conv_b, lru_wa, lru_ba, lru_wx, lru_bx,
         lru_lambda, lru_norm_g, w_out, norm2_g, ffn_w_gate, ffn_w_up, ffn_w_down, final_g):
    f32 = np.float32
    x = np.asarray(x, f32)
    w_in_t = np.stack([_tile_w(np.asarray(w_in[l], f32), KC) for l in range(DEPTH)])
    w_out_t = np.stack([_tile_w(np.asarray(w_out[l], f32), KC) for l in range(DEPTH)])
    w_gate_t = np.stack([_tile_w(np.asarray(ffn_w_gate[l], f32), KC) for l in range(DEPTH)])
    w_up_t = np.stack([_tile_w(np.asarray(ffn_w_up[l], f32), KC) for l in range(DEPTH)])
    wd = np.asarray(ffn_w_down, f32)
    w_down_t = np.stack([np.stack([_tile_w(wd[l, hf * FH * 128:(hf + 1) * FH * 128], FH) for hf in range(FQ)])
                         for l in range(DEPTH)])
    wgates = np.zeros((DEPTH, 128, 16, 128), f32)
    for l in range(DEPTH):
        for gi, wsrc in enumerate((lru_wa, lru_wx)):
            ws = np.asarray(wsrc[l], f32)
            for c in range(8):
                wgates[l, 0:64, gi * 8 + c, 0:64] = ws[2 * c]
                wgates[l, 64:128, gi * 8 + c, 64:128] = ws[2 * c + 1]
    pvec = np.zeros((128, PV_TOT), f32)

    def colmaj(v, n):
        return np.asarray(v, f32).reshape(n, 128).T

    for l in range(DEPTH):
        pb = PV_L * l
        pvec[:, pb + PV_N1G:pb + PV_N1G + 16] = colmaj(norm1_g[l], 16)
        pvec[:, pb + PV_N2G:pb + PV_N2G + 16] = colmaj(norm2_g[l], 16)
        pvec[:, pb + PV_GNG:pb + PV_GNG + 8] = colmaj(ret_gn_g[l], 8)
        for j in range(4):
            pvec[:, pb + PV_CW + j * 8:pb + PV_CW + (j + 1) * 8] = colmaj(lru_conv_w[l][j], 8)
        pvec[:, pb + PV_CB:pb + PV_CB + 8] = colmaj(lru_conv_b[l], 8)
        pvec[:, pb + PV_BA:pb + PV_BA + 8] = colmaj(lru_ba[l], 8)
        pvec[:, pb + PV_BX:pb + PV_BX + 8] = colmaj(lru_bx[l], 8)
        pvec[:, pb + PV_LAM:pb + PV_LAM + 8] = colmaj(lru_lambda[l], 8)
        pvec[:, pb + PV_LNG:pb + PV_LNG + 8] = colmaj(lru_norm_g[l], 8)
    pvec[:, PV_FINAL:PV_FINAL + 16] = colmaj(final_g, 16)
    tab, cmat, csml, rot = _host_tables()

    def x_tiles(b):
        xb_ = x[b]
        return np.ascontiguousarray(xb_.reshape(NT, T, KC, 128).transpose(0, 3, 2, 1))

    shared = dict(w_in=w_in_t, w_out=w_out_t, w_gate=w_gate_t, w_up=w_up_t, w_down=w_down_t,
                  w_gates=wgates, pvec=pvec, rot=rot, tab=tab, cmat=cmat, csml=csml)
    return shared, x_tiles


def kernel(**inputs):
    f32 = np.float32
    shared, x_tiles = prep(**inputs)
    B = np.asarray(inputs["x"]).shape[0]
    busy = [0, 1, 4, 5]
    zeros = {k: np.zeros_like(v) for k, v in shared.items()}
    zx = np.zeros((NT, 128, KC, T), f32)
    in_maps = []
    for c in range(NCORES):
        if c in busy:
            m = dict(shared)
            m["x_in"] = x_tiles(busy.index(c))
        else:
            m = dict(zeros)
            m["x_in"] = zx
        in_maps.append(m)
    nc = build_program()
    res = run_bass_kernel_spmd(nc, in_maps, core_ids=list(range(NCORES)))
    out = np.zeros((B, NT * T, D), f32)
    for b in range(B):
        yt = np.asarray(res.results[busy[b]]["y_out"], f32)
        out[b] = yt.transpose(0, 3, 2, 1).reshape(NT * T, D)
    return out
```

```python
import os
import numpy as np
import concourse.bass as bass
import concourse.mybir as mybir
from concourse.bass_utils import run_bass_kernel_spmd

F32 = mybir.dt.float32
BF16 = mybir.dt.bfloat16
AF = mybir.ActivationFunctionType
ALU = mybir.AluOpType

D = 2048
KC = 16
T = 1024
NT = 2
DEPTH = 2
NH = 8
DK = 128
CH = 64
DFF = 5632
FH = 11
FQ = 4
NB = 4
EPS = 1e-6
NCORES = 8

PV_N1G, PV_N2G, PV_GNG, PV_CW, PV_CB, PV_BA, PV_BX, PV_LAM, PV_LNG = 0, 16, 32, 40, 72, 80, 88, 96, 104
PV_L = 112
PV_FINAL = PV_L * DEPTH
PV_TOT = PV_FINAL + 16


class Sched:
    def __init__(self):
        self.ops = {e: [] for e in ("pe", "act", "dve", "pool", "sp")}
        self.cnt = {}
        self.step = {}
        self.seen = {e: {} for e in self.ops}
        self.lastw = {}
        self.readers = {}
        self.small_ev = set()

    def _sem(self, name, step):
        if name not in self.cnt:
            self.cnt[name] = 0
            self.step[name] = step

    def _deps(self, eng, reads, writes, small=False):
        waits = {}

        def need(ev, war=False):
            s, c = ev
            if s == eng and eng == "pe":
                return
            if c > self.seen[eng].get(s, 0):
                waits[s] = max(waits.get(s, 0), c)

        for k in reads:
            if k in self.lastw:
                need(self.lastw[k])
        for k in writes:
            if k in self.lastw:
                need(self.lastw[k])
            for r in self.readers.get(k, ()):
                need(r, war=True)
        for s, c in waits.items():
            self.seen[eng][s] = c
        return waits

    def _commit(self, ev, reads, writes):
        for k in writes:
            self.lastw[k] = ev
            self.readers[k] = []
        for k in reads:
            self.readers.setdefault(k, []).append(ev)

    def op(self, eng, fn, reads=(), writes=(), small=False):
        self._sem(eng, 1)
        waits = self._deps(eng, reads, writes, small)
        self.cnt[eng] += 1
        ev = (eng, self.cnt[eng])
        if small:
            self.small_ev.add(ev)
        self.ops[eng].append((waits, fn, eng, 1))
        self._commit(ev, reads, writes)
        return ev

    def finalize(self):
        engs = set(self.ops.keys())
        waited = {e: set() for e in engs}
        for e in engs:
            for waits, fn, semname, inc in self.ops[e]:
                for s_, c_ in waits.items():
                    if s_ in engs:
                        waited[s_].add(c_)
        rank = {}
        for e in engs:
            rank[e] = {c: i + 1 for i, c in enumerate(sorted(waited[e]))}
        out = {e: [] for e in engs}
        for e in engs:
            idx = 0
            for waits, fn, semname, inc in self.ops[e]:
                w2 = {s_: (rank[s_][c_] if s_ in engs else c_) for s_, c_ in waits.items()}
                if semname == e:
                    idx += 1
                    out[e].append((w2, fn, semname, 1 if idx in waited[e] else 0))
                else:
                    out[e].append((w2, fn, semname, inc))
        self.ops = out

    def dma(self, eng, fn, sem, reads=(), writes=()):
        self._sem(sem, 16)
        waits = self._deps(eng, reads, writes)
        self.cnt[sem] += 16
        ev = (sem, self.cnt[sem])
        self.ops[eng].append((waits, fn, sem, 16))
        self._commit(ev, reads, writes)
        return ev

    def wait_all(self, eng, keys):
        waits = self._deps(eng, keys, ())
        self.ops[eng].append((waits, None, None, 0))


def build_program(nt=NT, depth=DEPTH, phases=("lru", "ret", "outp", "ffn")):
    nc = bass.Bass("TRN2", target_bir_lowering=False)
    S = Sched()
    rstage = 4
    rsub = int(os.environ.get("RSUB", "9"))
    for _p in phases:
        if _p.startswith("rstage"):
            rstage = int(_p[6:])

    x_in = nc.dram_tensor("x_in", [NT, 128, KC, T], F32, kind="ExternalInput").ap()
    y_out = nc.dram_tensor("y_out", [NT, 128, KC, T], F32, kind="ExternalOutput").ap()
    WBF = os.environ.get("WBF16", "0") == "1"
    WDT = BF16 if WBF else F32
    w_in = nc.dram_tensor("w_in", [DEPTH, 48, 128, KC, 128], WDT, kind="ExternalInput").ap()
    w_out = nc.dram_tensor("w_out", [DEPTH, 16, 128, KC, 128], WDT, kind="ExternalInput").ap()
    w_gate = nc.dram_tensor("w_gate", [DEPTH, 44, 128, KC, 128], WDT, kind="ExternalInput").ap()
    w_up = nc.dram_tensor("w_up", [DEPTH, 44, 128, KC, 128], WDT, kind="ExternalInput").ap()
    w_down = nc.dram_tensor("w_down", [DEPTH, FQ, 16, 128, FH, 128], WDT, kind="ExternalInput").ap()
    w_gates = nc.dram_tensor("w_gates", [DEPTH, 128, 16, 128], F32, kind="ExternalInput").ap()
    pvec_d = nc.dram_tensor("pvec", [128, PV_TOT], F32, kind="ExternalInput").ap()
    rot_d = nc.dram_tensor("rot", [NT, 128, 2, T], F32, kind="ExternalInput").ap()
    tab_d = nc.dram_tensor("tab", [128, 2, NH, 128], F32, kind="ExternalInput").ap()
    cmat_d = nc.dram_tensor("cmat", [128, 3, 128], F32, kind="ExternalInput").ap()
    csml_d = nc.dram_tensor("csml", [128, 16], F32, kind="ExternalInput").ap()

    import contextlib
    with contextlib.ExitStack() as es:
        def sb(name, shape, dt):
            return es.enter_context(nc.sbuf_tensor(name, shape, dt))

        xT = sb("xT", [128, KC, T], F32)
        bufA = sb("bufA", [128, KC, T], BF16)
        bufB = sb("bufB", [128, KC, T], BF16)
        wbuf = [sb("wbuf%d" % i, [128, KC, 128], BF16) for i in range(NB)]
        wg_sb = sb("wg_sb", [128, 16, 128], BF16)
        pvec = sb("pvec_sb", [128, PV_TOT], F32)
        rot = sb("rot_sb", [128, 2, T], BF16)
        tab = sb("tab_sb", [128, 2, NH, 128], BF16)
        cmat = sb("cmat_sb", [128, 3, 128], BF16)
        csml = sb("csml_sb", [128, 16], F32)
        rstd = sb("rstd", [128, T], F32)
        sqb = [sb("sqb%d" % i, [128, T], BF16) for i in range(2)]
        Sst = sb("Sst", [128, DEPTH, NH, 128], F32)
        hst = sb("hst", [128, DEPTH, 8], F32)
        tail = sb("tail", [128, DEPTH, 8, 3], F32)
        clv = sb("clv", [128, 8], F32)
        spt = [sb("spt%d" % i, [128, 8], F32) for i in range(4)]
        f5 = [sb("f5_%d" % i, [128, 512], F32) for i in range(8)]
        xbp = sb("xbp", [128, T + 3], F32)
        b1 = [sb("b1_%d" % i, [128, T], BF16) for i in range(5)]
        small = sb("small", [128, T], BF16)
        onall = small
        bnst = sb("bnst", [128, 8, 6], F32)
        bnag = sb("bnag", [128, 8, 2], F32)
        grs = sb("grs", [128, 8], F32)
        yacc = rstd

        psA = es.enter_context(nc.psum_tensor("psA", [128, 6, 512], F32))
        psT = es.enter_context(nc.psum_tensor("psT", [128, 2, 1024], BF16))

        sems = {}

        def getsem(name):
            if name not in sems:
                sems[name] = es.enter_context(nc.semaphore("s_" + name))
            return sems[name]

        def pv(col, n=1):
            return pvec[:, col:col + n]

        eps_ap = csml[:, 8:9]
        one_ap = csml[:, 9:10]
        ident = cmat[:, 0, :]
        ones_d = cmat[:, 1, :]
        ones_l = cmat[:, 2, :]

        plan = []
        for tile in range(nt):
            for l in range(depth):
                if "lru" in phases:
                    for c in range(8):
                        plan.append((w_in[l, 32 + c], KC))
                        plan.append((w_in[l, 40 + c], KC))
                if "ret" in phases:
                    for h in range(NH):
                        for base in ((8,), ((8,), (8, 16), (8, 16, 0))[min(rsub, 3) - 1] if rstage == 2 else (8, 16, 0), (8, 16, 0), (8, 16, 0, 24))[rstage - 1]:
                            plan.append((w_in[l, base + h], KC))
                if "outp" in phases:
                    for m in range(16):
                        plan.append((w_out[l, m], KC))
                if "ffn" in phases:
                    for half in range(FQ):
                        for m in range(FH):
                            plan.append((w_gate[l, half * FH + m], KC))
                            plan.append((w_up[l, half * FH + m], KC))
                        for m in range(16):
                            plan.append((w_down[l, half, m], FH))
        wstate = {"issued": 0, "next": 0}

        def w_issue(upto):
            upto = min(upto, len(plan))
            while wstate["issued"] < upto:
                n = wstate["issued"]
                src, kcs = plan[n]
                bi = n % NB
                dst = wbuf[bi][:, 0:kcs, :]
                S.dma("sp" if WBF else "pool", (lambda e, dst=dst, src=src: e.dma_start(out=dst, in_=src)),
                      "w%d" % bi, writes=[("w", bi)])
                wstate["issued"] += 1

        def w_next(kcs):
            n = wstate["next"]
            assert plan[n][1] == kcs
            w_issue(n + NB)
            wstate["next"] += 1
            bi = n % NB
            return wbuf[bi], ("wr", bi)

        pstate = {"bank": 0, "tslot": 0}

        def nextbank():
            b = pstate["bank"]
            pstate["bank"] = (b + 1) % 6
            return b

        def proj(rhs_buf, rhs_key, kcs, evac):
            wb, wkey = w_next(kcs)
            banks = (nextbank(), nextbank())

            def mm(e, wb=wb, banks=banks):
                ins = None
                for kc in range(kcs):
                    for th in range(2):
                        ins = e.matmul(psA[:, banks[th], :], lhsT=wb[:, kc, :],
                                       rhs=rhs_buf[:, kc, th * 512:(th + 1) * 512],
                                       start=(kc == 0), stop=(kc == kcs - 1))
                return ins
            S.op("pe", mm, reads=[wkey, rhs_key], writes=[("ps", banks[0]), ("ps", banks[1])])
            for th in range(2):
                evac(th, psA[:, banks[th], :], ("ps", banks[th]))

        def mslot():
            b = nextbank()
            return psA[:, b, 0:128], ("ps", b)

        def tslot():
            s = pstate["tslot"]
            pstate["tslot"] = (s + 1) % 2
            return psT[:, s, 0:512], ("ps", 6 + s)

        def rmsnorm(gcol, out_fn):
            nb = (nextbank(), nextbank())
            for kc in range(KC):
                sq = sqb[kc % 2]
                S.op("act", (lambda e, sq=sq, kc=kc: e.activation(out=sq[:, :], in_=xT[:, kc, :], func=AF.Square)),
                     reads=[("x", kc)], writes=[("sqb", kc % 2)])

                def mm(e, sq=sq, kc=kc):
                    ins = None
                    for th in range(2):
                        ins = e.matmul(psA[:, nb[th], :], lhsT=ones_d, rhs=sq[:, th * 512:(th + 1) * 512],
                                       start=(kc == 0), stop=(kc == KC - 1))
                    return ins
                S.op("pe", mm, reads=[("sqb", kc % 2), "cmat"], writes=[("ps", nb[0]), ("ps", nb[1])])
            for th in range(2):
                S.op("act", (lambda e, th=th: e.activation(out=rstd[:, th * 512:(th + 1) * 512], in_=psA[:, nb[th], :],
                                                           func=AF.Sqrt, bias=eps_ap, scale=1.0)),
                     reads=[("ps", nb[th]), "csml"], writes=[("rstd", th)])
                S.op("dve", (lambda e, th=th: e.reciprocal(out=rstd[:, th * 512:(th + 1) * 512],
                                                           in_=rstd[:, th * 512:(th + 1) * 512])),
                     reads=[("rstd", th)], writes=[("rstd", th)])
            for kc in range(KC):
                out_fn(kc, gcol + kc)

        def norm_to_bufA(kc, gc):
            S.op("dve", (lambda e, kc=kc, gc=gc: e.scalar_tensor_tensor(
                out=bufA[:, kc, :], in0=xT[:, kc, :], scalar=pv(gc), in1=rstd[:, :],
                op0=ALU.mult, op1=ALU.mult)),
                reads=[("x", kc), ("rstd", 0), ("rstd", 1), "pvec"], writes=[("A", kc)])

        def norm_inplace(kc, gc):
            S.op("dve", (lambda e, kc=kc, gc=gc: e.scalar_tensor_tensor(
                out=xT[:, kc, :], in0=xT[:, kc, :], scalar=pv(gc), in1=rstd[:, :],
                op0=ALU.mult, op1=ALU.mult)),
                reads=[("x", kc), ("rstd", 0), ("rstd", 1), "pvec"], writes=[("x", kc)])

        A_keys = [("A", kc) for kc in range(KC)]

        class AKey:
            pass

        S.dma("pool", lambda e: e.dma_start(out=pvec[:, :], in_=pvec_d[:, :]), "ld0", writes=["pvec"])
        S.dma("pool", lambda e: e.dma_start(out=tab[:, :, :, :], in_=tab_d[:, :, :, :]), "ld1", writes=["tab"])
        S.dma("pool", lambda e: e.dma_start(out=cmat[:, :, :], in_=cmat_d[:, :, :]), "ld2", writes=["cmat"])
        S.dma("pool", lambda e: e.dma_start(out=csml[:, :], in_=csml_d[:, :]), "ld3", writes=["csml"])
        S.op("dve", lambda e: e.memset(Sst[:, :, :, :], 0.0), writes=["Sst"])
        S.op("dve", lambda e: e.memset(hst[:, :, :], 0.0), writes=["hst"])
        S.op("dve", lambda e: e.memset(tail[:, :, :, :], 0.0), writes=["tail"])

        for tile in range(nt):
            S.dma("sp", (lambda e, tile=tile: e.dma_start(out=xT[:, :, :], in_=x_in[tile])), "ldx",
                  writes=[("x", kc) for kc in range(KC)])
            S.dma("pool", (lambda e, tile=tile: e.dma_start(out=rot[:, :, :], in_=rot_d[tile])), "ldr",
                  writes=["rot"])
            for l in range(depth):
                pb = PV_L * l
                S.dma("pool", (lambda e, l=l: e.dma_start(out=wg_sb[:, :, :], in_=w_gates[l])), "ldg",
                      writes=["wg"])
                lam = pv(pb + PV_LAM, 8)
                s0, s1, s2, s3 = spt
                S.op("dve", lambda e, lam=lam: e.tensor_scalar(out=s0[:, :], in0=lam, scalar1=-1.0, scalar2=None, op0=ALU.mult),
                     reads=["pvec"], writes=["s0"], small=True)
                S.op("dve", lambda e, lam=lam: e.tensor_tensor(out=s0[:, :], in0=s0[:, :], in1=lam, op=ALU.max),
                     reads=["pvec", "s0"], writes=["s0"], small=True)
                S.op("act", lambda e: e.activation(out=s0[:, :], in_=s0[:, :], func=AF.Exp, scale=-1.0),
                     reads=["s0"], writes=["s0"], small=True)
                S.op("dve", lambda e: e.tensor_scalar(out=s1[:, :], in0=s0[:, :], scalar1=2.0, scalar2=None, op0=ALU.add),
                     reads=["s0"], writes=["s1"], small=True)
                S.op("dve", lambda e: e.reciprocal(out=s1[:, :], in_=s1[:, :]), reads=["s1"], writes=["s1"], small=True)
                S.op("dve", lambda e: e.tensor_tensor(out=s1[:, :], in0=s1[:, :], in1=s0[:, :], op=ALU.mult),
                     reads=["s1", "s0"], writes=["s1"], small=True)
                S.op("dve", lambda e: e.tensor_tensor(out=s2[:, :], in0=s1[:, :], in1=s1[:, :], op=ALU.mult),
                     reads=["s1"], writes=["s2"], small=True)
                S.op("dve", lambda e: e.tensor_scalar(out=s3[:, :], in0=s2[:, :], scalar1=1.0 / 13.0, scalar2=1.0 / 11.0,
                                                      op0=ALU.mult, op1=ALU.add), reads=["s2"], writes=["s3"], small=True)
                for kk in (9.0, 7.0, 5.0, 3.0, 1.0):
                    S.op("dve", lambda e: e.tensor_tensor(out=s3[:, :], in0=s3[:, :], in1=s2[:, :], op=ALU.mult),
                         reads=["s3", "s2"], writes=["s3"], small=True)
                    S.op("dve", lambda e, kk=kk: e.tensor_scalar(out=s3[:, :], in0=s3[:, :], scalar1=1.0 / kk, scalar2=None,
                                                                 op0=ALU.add), reads=["s3"], writes=["s3"], small=True)
                S.op("dve", lambda e: e.tensor_tensor(out=s3[:, :], in0=s3[:, :], in1=s1[:, :], op=ALU.mult),
                     reads=["s3", "s1"], writes=["s3"], small=True)
                S.op("dve", lambda e, lam=lam: e.tensor_scalar(out=s0[:, :], in0=lam, scalar1=-1.0, scalar2=0.0,
                                                               op0=ALU.mult, op1=ALU.max), reads=["pvec", "s0"], writes=["s0"], small=True)
                S.op("dve", lambda e: e.scalar_tensor_tensor(out=s0[:, :], in0=s3[:, :], scalar=2.0, in1=s0[:, :],
                                                             op0=ALU.mult, op1=ALU.add), reads=["s3", "s0"], writes=["s0"], small=True)
                S.op("dve", lambda e: e.tensor_scalar(out=clv[:, :], in0=s0[:, :], scalar1=-8.0, scalar2=None, op0=ALU.mult),
                     reads=["s0"], writes=["clv"], small=True)

                rmsnorm(pb + PV_N1G, norm_to_bufA)

                if "lru" not in phases:
                    S.op("dve", lambda e: e.memset(bufB[:, 8:16, :], 0.0), writes=[("B", m_, t_) for m_ in range(8, 16) for t_ in range(2)])
                for c in (range(8) if "lru" in phases else ()):
                    S.op("act", (lambda e, c=c, l=l: e.activation(out=xbp[:, 0:3], in_=tail[:, l, c, :], func=AF.Copy)),
                         reads=["tail"], writes=[("xbp", -1)], small=True)

                    def ev_xb(th, ps, pk, c=c):
                        S.op("act", (lambda e, th=th, ps=ps: e.activation(out=xbp[:, 3 + th * 512:3 + (th + 1) * 512],
                                                                          in_=ps, func=AF.Copy)),
                             reads=[pk], writes=[("xbp", th)])
                    proj(bufA, "Aall", KC, ev_xb)
                    S.op("act", (lambda e, c=c, l=l: e.activation(out=tail[:, l, c, :], in_=xbp[:, T:T + 3], func=AF.Copy)),
                         reads=[("xbp", 1)], writes=["tail"], small=True)

                    steps = {0: [], 1: []}

                    def ev_yb(th, ps, pk, c=c, l=l, pb=pb, steps=steps):
                        def emit(*a_, **k_):
                            steps[th].append((a_, k_))
                        gb = b1[2 + th]
                        gk = ("b1", 2 + th)
                        xc, t1, t2, t3 = f5[4 * th + 0], f5[4 * th + 1], f5[4 * th + 2], f5[4 * th + 3]
                        kx, k1, k2, k3 = [("f5", 4 * th + i) for i in range(4)]
                        xcb = b1[th]
                        o0 = th * 512
                        emit("act", (lambda e: e.activation(out=t3[:, :], in_=ps, func=AF.Square)),
                             reads=[pk, k3], writes=[k3])
                        emit("dve", (lambda e: e.tensor_scalar(out=t3[:, :], in0=t3[:, :], scalar1=0.044715, scalar2=1.0,
                                                               op0=ALU.mult, op1=ALU.add)), reads=[k3], writes=[k3])
                        emit("dve", (lambda e: e.tensor_tensor(out=t3[:, :], in0=ps, in1=t3[:, :], op=ALU.mult)),
                             reads=[pk, k3], writes=[k3])
                        emit("act", (lambda e: e.activation(out=t3[:, :], in_=t3[:, :], func=AF.Sigmoid, scale=1.5957691216057308)),
                             reads=[k3], writes=[k3])
                        emit("dve", (lambda e: e.tensor_tensor(out=gb[:, 0:512], in0=ps, in1=t3[:, :], op=ALU.mult)),
                             reads=[pk, k3], writes=[gk])
                        cw = pb + PV_CW
                        xdeps = [("xbp", -1), ("xbp", 0), ("xbp", 1)]
                        emit("act", (lambda e: e.activation(out=xc[:, :], in_=xbp[:, o0 + 3:o0 + 515], func=AF.Identity,
                                                            bias=pv(pb + PV_CB + c), scale=pv(cw + 3 * 8 + c))),
                             reads=xdeps + ["pvec"], writes=[kx])
                        for j in range(3):
                            emit("dve", (lambda e, j=j: e.scalar_tensor_tensor(
                                out=xc[:, :], in0=xbp[:, o0 + j:o0 + j + 512], scalar=pv(cw + j * 8 + c), in1=xc[:, :],
                                op0=ALU.mult, op1=ALU.add)), reads=xdeps + [kx, "pvec"], writes=[kx])
                        emit("act", (lambda e: e.activation(out=xcb[:, 0:512], in_=xc[:, :], func=AF.Copy)),
                             reads=[kx], writes=[("b1", th)])
                        rb_, ib_ = nextbank(), nextbank()
                        rps = psA[:, rb_, :]
                        ips = psA[:, ib_, :]
                        rk, ik = ("ps", rb_), ("ps", ib_)
                        emit("pe", (lambda e: e.matmul(rps, lhsT=wg_sb[:, c, :], rhs=xcb[:, 0:512], start=True, stop=True)),
                             reads=[("b1", th), "wg"], writes=[rk])
                        emit("pe", (lambda e: e.matmul(ips, lhsT=wg_sb[:, 8 + c, :], rhs=xcb[:, 0:512], start=True, stop=True)),
                             reads=[("b1", th), "wg"], writes=[ik])
                        emit("act", (lambda e: e.activation(out=t1[:, :], in_=rps, func=AF.Sigmoid, bias=pv(pb + PV_BA + c))),
                             reads=[rk, "pvec"], writes=[k1])
                        emit("act", (lambda e: e.activation(out=t1[:, :], in_=t1[:, :], func=AF.Exp, scale=clv[:, c:c + 1])),
                             reads=[k1, "clv"], writes=[k1])
                        emit("act", (lambda e: e.activation(out=t3[:, :], in_=ips, func=AF.Sigmoid, bias=pv(pb + PV_BX + c))),
                             reads=[ik, "pvec"], writes=[k3])
                        emit("dve", (lambda e: e.tensor_tensor(out=t2[:, :], in0=t1[:, :], in1=t1[:, :], op=ALU.mult)),
                             reads=[k1], writes=[k2])
                        emit("act", (lambda e: e.activation(out=t2[:, :], in_=t2[:, :], func=AF.Sqrt, bias=one_ap, scale=-1.0)),
                             reads=[k2, "csml"], writes=[k2])
                        emit("dve", (lambda e: e.tensor_tensor(out=t3[:, :], in0=t3[:, :], in1=xc[:, :], op=ALU.mult)),
                             reads=[k3, kx], writes=[k3])
                        emit("dve", (lambda e: e.tensor_tensor(out=t3[:, :], in0=t3[:, :], in1=t2[:, :], op=ALU.mult)),
                             reads=[k3, k2], writes=[k3])
                        emit("dve", (lambda e: e.tensor_tensor_scan(out=t2[:, :], data0=t1[:, :], data1=t3[:, :],
                                                                    initial=hst[:, l, c:c + 1], op0=ALU.mult, op1=ALU.add)),
                             reads=[k1, k3, "hst", k2], writes=[k2])
                        emit("dve", (lambda e: e.tensor_copy(out=hst[:, l, c:c + 1], in_=t2[:, 511:512])),
                             reads=[k2], writes=["hst"], small=True)
                        emit("dve", (lambda e: e.tensor_tensor(out=t2[:, :], in0=t2[:, :], in1=gb[:, 0:512], op=ALU.mult)),
                             reads=[k2, gk], writes=[k2])
                        emit("act", (lambda e: e.activation(out=t3[:, :], in_=t2[:, :], func=AF.Square)),
                             reads=[k2], writes=[k3])
                        if c == 0:
                            emit("dve", (lambda e: e.tensor_copy(out=yacc[:, o0:o0 + 512], in_=t3[:, :])),
                                 reads=[k3], writes=[("rstd", th)])
                        else:
                            emit("dve", (lambda e: e.tensor_tensor(out=yacc[:, o0:o0 + 512], in0=yacc[:, o0:o0 + 512],
                                                                   in1=t3[:, :], op=ALU.add)),
                                 reads=[k3, ("rstd", th)], writes=[("rstd", th)])
                        emit("act", (lambda e: e.activation(out=bufB[:, 8 + c, o0:o0 + 512], in_=t2[:, :], func=AF.Identity,
                                                            scale=pv(pb + PV_LNG + c))),
                             reads=[k2, "pvec"], writes=[("B", 8 + c, th)])
                    proj(bufA, "Aall", KC, ev_yb)
                    LAG = 3
                    for i_ in range(max(len(steps[0]), len(steps[1]) + LAG)):
                        if i_ < len(steps[0]):
                            S.op(*steps[0][i_][0], **steps[0][i_][1])
                        if 0 <= i_ - LAG < len(steps[1]):
                            S.op(*steps[1][i_ - LAG][0], **steps[1][i_ - LAG][1])
                for th in (range(2) if "lru" in phases else ()):
                    o0 = th * 512
                    lb_ = nextbank()
                    S.op("act", (lambda e, th=th, o0=o0: e.activation(out=sqb[0][:, o0:o0 + 512], in_=yacc[:, o0:o0 + 512], func=AF.Copy)),
                         reads=[("rstd", th)], writes=[("sqb", 0)])
                    S.op("pe", (lambda e, th=th, o0=o0, lb_=lb_: e.matmul(psA[:, lb_, :], lhsT=ones_l, rhs=sqb[0][:, o0:o0 + 512],
                                                                           start=True, stop=True)),
                         reads=[("sqb", 0), "cmat"], writes=[("ps", lb_)])
                    S.op("act", (lambda e, th=th, o0=o0, lb_=lb_: e.activation(out=rstd[:, o0:o0 + 512], in_=psA[:, lb_, :],
                                                                               func=AF.Sqrt, bias=eps_ap, scale=1.0)),
                         reads=[("ps", lb_), "csml"], writes=[("rstd", th)])
                    S.op("dve", (lambda e, o0=o0: e.reciprocal(out=rstd[:, o0:o0 + 512], in_=rstd[:, o0:o0 + 512])),
                         reads=[("rstd", th)], writes=[("rstd", th)])
                for c in (range(8) if "lru" in phases else ()):
                    S.op("dve", (lambda e, c=c: e.tensor_tensor(out=bufB[:, 8 + c, :], in0=bufB[:, 8 + c, :], in1=rstd[:, :],
                                                                op=ALU.mult)),
                         reads=[("B", 8 + c, 0), ("B", 8 + c, 1), ("rstd", 0), ("rstd", 1)],
                         writes=[("B", 8 + c, 0), ("B", 8 + c, 1)])

                kT, vT, qT, ktok, vtok = b1
                qd = vT
                gsil = sqb[1]
                rt1 = [f5[0], f5[4]]
                rt2 = [f5[1], f5[5]]
                rk1 = [("f5", 0), ("f5", 4)]
                rk2 = [("f5", 1), ("f5", 5)]

                def rotary_evac(dst, dkey):
                    def ev(th, ps, pk):
                        o0 = th * 512
                        t1, t2, k1, k2 = rt1[th], rt2[th], rk1[th], rk2[th]
                        S.op("dve", (lambda e: e.tensor_tensor(out=t1[:, :], in0=ps, in1=rot[:, 0, o0:o0 + 512], op=ALU.mult)),
                             reads=[pk, "rot"], writes=[k1])
                        S.op("dve", (lambda e: e.tensor_tensor(out=t2[0:64, :], in0=ps[64:128, :], in1=rot[0:64, 1, o0:o0 + 512],
                                                               op=ALU.mult)), reads=[pk, "rot"], writes=[k2])
                        S.op("dve", (lambda e: e.tensor_tensor(out=t2[64:128, :], in0=ps[0:64, :], in1=rot[64:128, 1, o0:o0 + 512],
                                                               op=ALU.mult)), reads=[pk, "rot", k2], writes=[k2])
                        S.op("dve", (lambda e: e.tensor_tensor(out=dst[:, o0:o0 + 512], in0=t1[:, :], in1=t2[:, :], op=ALU.add)),
                             reads=[k1, k2], writes=[("b1", dkey, th)])
                    return ev

                if "ret" not in phases:
                    S.op("dve", lambda e: e.memset(bufB[:, 0:8, :], 0.0), writes=[("B", m_, t_) for m_ in range(8) for t_ in range(2)])
                for h in (range(NH) if "ret" in phases else ()):
                    gam = 1.0 - 2.0 ** (-5.0 - h)
                    proj(bufA, "Aall", KC, rotary_evac(kT, 0))
                    def ev_v(th, ps, pk):
                        S.op("act", (lambda e: e.activation(out=vT[:, th * 512:(th + 1) * 512], in_=ps, func=AF.Copy)),
                             reads=[pk], writes=[("b1", 1, th)])
                    proj(bufA, "Aall", KC, ev_v)
                    proj(bufA, "Aall", KC, rotary_evac(qT, 2))
                    def ev_g(th, ps, pk):
                        tg, tgk = f5[6 + th], ("f5", 6 + th)
                        S.op("act", (lambda e: e.activation(out=tg[:, :], in_=ps, func=AF.Sigmoid)),
                             reads=[pk], writes=[tgk])
                        S.op("dve", (lambda e: e.tensor_tensor(out=gsil[:, th * 512:(th + 1) * 512], in0=ps, in1=tg[:, :], op=ALU.mult)),
                             reads=[pk, tgk], writes=[("sqb", 1, th)])
                    proj(bufA, "Aall", KC, ev_g)
                    for g4 in range(2):
                        tp, tk = tslot()

                        def trk(e, g4=g4, tp=tp):
                            ins = None
                            for jj in range(4):
                                j = g4 * 4 + jj
                                ins = e.transpose(tp[:, jj * 128:(jj + 1) * 128], kT[:, j * 128:(j + 1) * 128], ident)
                            return ins
                        S.op("pe", trk, reads=[("b1", 0, g4), "cmat"], writes=[tk])
                        S.op("act", (lambda e, g4=g4, tp=tp, h=h: e.activation(out=ktok[:, g4 * 512:(g4 + 1) * 512], in_=tp,
                                                                               func=AF.Identity, scale=csml[:, h:h + 1])),
                             reads=[tk, "csml"], writes=[("b1", 3, g4)])

                    for g4 in range(2):
                        tp, tk = tslot()

                        def trv(e, g4=g4, tp=tp):
                            ins = None
                            for jj in range(4):
                                j = g4 * 4 + jj
                                ins = e.transpose(tp[:, jj * 128:(jj + 1) * 128], vT[:, j * 128:(j + 1) * 128], ident)
                            return ins
                        S.op("pe", trv, reads=[("b1", 1, g4), "cmat"], writes=[tk])
                        S.op("act", (lambda e, g4=g4, tp=tp: e.activation(out=vtok[:, g4 * 512:(g4 + 1) * 512], in_=tp, func=AF.Copy)),
                             reads=[tk], writes=[("b1", 4, g4)])
                    for j in range(8):
                        S.op("dve", (lambda e, j=j, h=h: e.tensor_tensor(out=qd[:, j * 128:(j + 1) * 128],
                                                                         in0=qT[:, j * 128:(j + 1) * 128],
                                                                         in1=tab[:, 1, h, :], op=ALU.mult)),
                             reads=[("b1", 2, j // 4), "tab"], writes=[("b1", 1, j // 4)])
                    g128 = float(gam ** 128)
                    ub = (nextbank(), nextbank())

                    def mmu(e, ub=ub):
                        ins = None
                        for j in range(8):
                            js = slice(j * 128, (j + 1) * 128)
                            ins = e.matmul(psA[:, ub[j // 4], (j % 4) * 128:(j % 4 + 1) * 128], lhsT=ktok[:, js], rhs=vtok[:, js],
                                           start=True, stop=True)
                        return ins
                    S.op("pe", mmu, reads=[("b1", 3, 0), ("b1", 3, 1), ("b1", 4, 0), ("b1", 4, 1)],
                         writes=[("ps", ub[0]), ("ps", ub[1])])
                    sbk = (nextbank(), nextbank())

                    def mms(e, sbk=sbk):
                        ins = None
                        for j in range(8):
                            js = slice(j * 128, (j + 1) * 128)
                            ins = e.matmul(psA[:, sbk[j // 4], (j % 4) * 128:(j % 4 + 1) * 128], lhsT=kT[:, js], rhs=qT[:, js],
                                           start=True, stop=True)
                        return ins
                    S.op("pe", mms, reads=[("b1", 0, 0), ("b1", 0, 1), ("b1", 2, 0), ("b1", 2, 1)],
                         writes=[("ps", sbk[0]), ("ps", sbk[1])])
                    for j in range(8):
                        S.op("dve", (lambda e, j=j, h=h, sbk=sbk: e.tensor_tensor(
                            out=small[:, j * 128:(j + 1) * 128], in0=psA[:, sbk[j // 4], (j % 4) * 128:(j % 4 + 1) * 128],
                            in1=tab[:, 0, h, :], op=ALU.mult)),
                            reads=[("ps", sbk[j // 4]), "tab"], writes=[("small", j // 4)])
                    def sslot(j):
                        return f5[6 + j // 4][:, (j % 4) * 128:(j % 4 + 1) * 128]
                    S.op("act", (lambda e, h=h, l=l: e.activation(out=sslot(0), in_=Sst[:, l, h, :], func=AF.Copy)),
                         reads=["Sst"], writes=[("f5", 6)])
                    for j in range(8):
                        dst = sslot(j + 1) if j < 7 else Sst[:, l, h, :]
                        S.op("dve", (lambda e, j=j, dst=dst, ub=ub, g128=g128: e.scalar_tensor_tensor(
                            out=dst, in0=sslot(j), scalar=g128,
                            in1=psA[:, ub[j // 4], (j % 4) * 128:(j % 4 + 1) * 128],
                            op0=ALU.mult, op1=ALU.add)),
                            reads=[("ps", ub[j // 4]), ("f5", 6 + j // 4)],
                            writes=[("f5", 6 + (j + 1) // 4)] if j < 7 else ["Sst"], small=True)
                    for g4 in range(2):
                        S.op("act", (lambda e, g4=g4: e.activation(out=kT[:, g4 * 512:(g4 + 1) * 512], in_=f5[6 + g4][:, :], func=AF.Copy)),
                             reads=[("f5", 6 + g4)], writes=[("b1", 0, g4)])
                    ob1 = (nextbank(), nextbank())

                    def mmo1(e, ob1=ob1):
                        ins = None
                        for j in range(8):
                            js = slice(j * 128, (j + 1) * 128)
                            ins = e.matmul(psA[:, ob1[j // 4], (j % 4) * 128:(j % 4 + 1) * 128], lhsT=small[:, js], rhs=vtok[:, js],
                                           start=True, stop=True)
                        return ins
                    S.op("pe", mmo1, reads=[("small", 0), ("small", 1), ("b1", 4, 0), ("b1", 4, 1)],
                         writes=[("ps", ob1[0]), ("ps", ob1[1])])
                    ob2 = (nextbank(), nextbank())

                    def mmo2(e, ob2=ob2):
                        ins = None
                        for j in range(8):
                            js = slice(j * 128, (j + 1) * 128)
                            ins = e.matmul(psA[:, ob2[j // 4], (j % 4) * 128:(j % 4 + 1) * 128], lhsT=qd[:, js], rhs=kT[:, js],
                                           start=True, stop=True)
                        return ins
                    S.op("pe", mmo2, reads=[("b1", 1, 0), ("b1", 1, 1), ("b1", 0, 0), ("b1", 0, 1)],
                         writes=[("ps", ob2[0]), ("ps", ob2[1])])
                    for g4 in range(2):
                        S.op("act", (lambda e, g4=g4, ob1=ob1: e.activation(out=f5[2 + g4][:, :], in_=psA[:, ob1[g4], :], func=AF.Copy)),
                             reads=[("ps", ob1[g4])], writes=[("f5", 2 + g4)])
                        S.op("dve", (lambda e, g4=g4, ob2=ob2: e.tensor_tensor(out=f5[2 + g4][:, :], in0=psA[:, ob2[g4], :],
                                                                               in1=f5[2 + g4][:, :], op=ALU.add)),
                             reads=[("ps", ob2[g4]), ("f5", 2 + g4)], writes=[("f5", 2 + g4)])
                    for j in range(8):
                        S.op("dve", (lambda e, j=j: e.bn_stats(out=bnst[:, j, :],
                                                               in_=f5[2 + j // 4][:, (j % 4) * 128:(j % 4 + 1) * 128])),
                             reads=[("f5", 2 + j // 4)], writes=[("bnst", j)])
                    for j in range(8):
                        S.op("dve", (lambda e, j=j: e.bn_aggr(out=bnag[:, j, :], in_=bnst[:, j, :])),
                             reads=[("bnst", j)], writes=[("bnag", j)], small=True)
                    bkeys = [("bnag", j) for j in range(8)]
                    S.op("act", (lambda e: e.activation(out=grs[:, :], in_=bnag[:, :, 1], func=AF.Sqrt, bias=eps_ap, scale=1.0)),
                         reads=bkeys + ["csml"], writes=["grs"], small=True)
                    S.op("dve", (lambda e: e.reciprocal(out=grs[:, :], in_=grs[:, :])), reads=["grs"], writes=["grs"], small=True)
                    for j in range(8):
                        S.op("dve", (lambda e, j=j: e.tensor_scalar(out=onall[:, j * 128:(j + 1) * 128],
                                                                    in0=f5[2 + j // 4][:, (j % 4) * 128:(j % 4 + 1) * 128],
                                                                    scalar1=bnag[:, j, 0:1], scalar2=grs[:, j:j + 1],
                                                                    op0=ALU.subtract, op1=ALU.mult)),
                             reads=[("f5", 2 + j // 4), ("bnag", j), "grs"], writes=[("small", j // 4)])
                    for g4 in range(2):
                        tp, tk = tslot()

                        def trn(e, g4=g4, tp=tp):
                            ins = None
                            for jj in range(4):
                                j = g4 * 4 + jj
                                ins = e.transpose(tp[:, jj * 128:(jj + 1) * 128], onall[:, j * 128:(j + 1) * 128], ident)
                            return ins
                        S.op("pe", trn, reads=[("small", g4), "cmat"], writes=[tk])
                        S.op("dve", (lambda e, g4=g4, tp=tp, h=h, pb=pb: e.scalar_tensor_tensor(
                            out=bufB[:, h, g4 * 512:(g4 + 1) * 512], in0=tp, scalar=pv(pb + PV_GNG + h),
                            in1=gsil[:, g4 * 512:(g4 + 1) * 512], op0=ALU.mult, op1=ALU.mult)),
                            reads=[tk, ("sqb", 1, g4), "pvec"], writes=[("B", h, g4)])

                def ev_res(mo):
                    def ev(th, ps, pk):
                        o0 = th * 512
                        S.op("dve", (lambda e: e.tensor_tensor(out=xT[:, mo, o0:o0 + 512], in0=ps, in1=xT[:, mo, o0:o0 + 512],
                                                               op=ALU.add)), reads=[pk, ("x", mo)], writes=[("x", mo)])
                    return ev
                for mo in (range(16) if "outp" in phases else ()):
                    proj(bufB, "Ball", KC, ev_res(mo))

                if "ffn" in phases:
                    rmsnorm(pb + PV_N2G, norm_to_bufA)
                for half in (range(FQ) if "ffn" in phases else ()):
                    for m in range(FH):
                        def ev_gate(th, ps, pk):
                            S.op("act", (lambda e: e.activation(out=f5[2 + 4 * th][:, :], in_=ps, func=AF.Sigmoid)),
                                 reads=[pk], writes=[("f5", 2 + 4 * th)])
                            S.op("dve", (lambda e: e.tensor_tensor(out=f5[2 + 4 * th][:, :], in0=ps, in1=f5[2 + 4 * th][:, :], op=ALU.mult)),
                                 reads=[pk, ("f5", 2 + 4 * th)], writes=[("f5", 2 + 4 * th)])
                        proj(bufA, "Aall", KC, ev_gate)

                        def ev_up(th, ps, pk, m=m):
                            o0 = th * 512
                            S.op("dve", (lambda e: e.tensor_tensor(out=bufB[:, m, o0:o0 + 512], in0=ps, in1=f5[2 + 4 * th][:, :],
                                                                   op=ALU.mult)), reads=[pk, ("f5", 2 + 4 * th)], writes=[("B", m, th)])
                        proj(bufA, "Aall", KC, ev_up)
                    for mo in range(16):
                        proj(bufB, "Ball", FH, ev_res(mo))

            rmsnorm(PV_FINAL, norm_inplace)
            S.dma("sp", (lambda e, tile=tile: e.dma_start(out=y_out[tile], in_=xT[:, :, :])), "st",
                  reads=[("x", kc) for kc in range(KC)])
        S.wait_all("sp", [])
        final_st = S.cnt["st"]

        engmap = {"pe": "tensor", "act": "scalar", "dve": "vector", "pool": "gpsimd", "sp": "sync"}
        if os.environ.get("DENSE", "0") != "1":
            S.finalize()
        for _n in list(S.cnt.keys()):
            getsem(_n)
        with nc.Block() as block:
            def make(engname):
                def body(e):
                    for waits, fn, semname, inc in S.ops[engname]:
                        for sname, val in waits.items():
                            e.wait_ge(getsem(sname), val)
                        if fn is None:
                            continue
                        ins = fn(e)
                        if inc:
                            ins.then_inc(getsem(semname), inc)
                    if engname == "sp":
                        e.wait_ge(getsem("st"), final_st)
                return body
            for engname, attr in engmap.items():
                getattr(block, attr)(make(engname))
    return nc


_orig_deps = Sched._deps
_orig_commit = Sched._commit


def _expand(keys):
    out = []
    for k in keys:
        if k == "Aall":
            out.extend(("A", kc) for kc in range(KC))
        elif isinstance(k, tuple) and k[0] == "wr":
            out.append(("w", k[1]))
        elif isinstance(k, tuple) and k[0] == "pmb":
            out.extend(("pm", (k[1] - 4) * 4 + i) for i in range(4))
        elif isinstance(k, tuple) and k[0] == "sqb" and len(k) == 2:
            out.extend([("sqb", k[1], 0), ("sqb", k[1], 1)])
        elif isinstance(k, tuple) and k[0] == "b1" and len(k) == 2:
            out.extend([("b1", k[1], 0), ("b1", k[1], 1)])
        elif k == "Ball":
            out.extend(("B", m, th) for m in range(KC) for th in range(2))
        else:
            out.append(k)
    return out


def _deps2(self, eng, reads, writes, small=False):
    return _orig_deps(self, eng, _expand(reads), _expand(writes), small)


def _commit2(self, ev, reads, writes):
    return _orig_commit(self, ev, _expand(reads), _expand(writes))


Sched._deps = _deps2
Sched._commit = _commit2


def _tile_w(w, kcs):
    K, N = w.shape
    assert K == kcs * 128
    return np.ascontiguousarray(w.reshape(kcs, 128, N // 128, 128).transpose(2, 1, 0, 3))


def _host_tables():
    f32 = np.float32
    H, C = NH, CH
    log_g = np.log1p(-np.exp2(-5.0 - np.arange(H, dtype=np.float64)))
    idx = np.arange(128, dtype=np.float64)
    a = idx[None, :]
    c = idx[:, None]
    same_or_prev = (np.floor(c / C) <= np.floor(a / C))
    mask = np.exp(log_g[:, None, None] * np.abs(a - c)[None]) * same_or_prev[None] * (DK ** -0.5)
    qdec = np.exp(log_g[:, None] * (idx + 1.0)[None, :])
    kdec = np.exp(log_g[:, None] * (127.0 - idx)[None, :]) * (DK ** -0.5)
    tab = np.zeros((128, 2, H, 128), f32)
    tab[:, 0] = mask.transpose(1, 0, 2)
    tab[:, 1] = np.broadcast_to(qdec[None], (128, H, 128))
    cmat = np.zeros((128, 3, 128), f32)
    cmat[:, 0] = np.eye(128)
    cmat[:, 1] = 1.0 / D
    cmat[:, 2] = 1.0 / 1024.0
    csml = np.zeros((128, 16), f32)
    csml[:, 0:8] = kdec.T
    csml[:, 8] = EPS
    csml[:, 9] = 1.0
    inv = 1.0 / (10000.0 ** (np.arange(0, DK, 2, dtype=np.float64) / DK))
    rot = np.zeros((NT, 128, 2, T), f32)
    for t in range(NT):
        pos = np.arange(t * T, (t + 1) * T, dtype=np.float64)
        ang = (pos[None, :].astype(np.float32) * inv[:, None].astype(np.float32)).astype(np.float32)
        cs, sn = np.cos(ang.astype(np.float64)), np.sin(ang.astype(np.float64))
        rot[t, 0:64, 0] = cs
        rot[t, 64:128, 0] = cs
        rot[t, 0:64, 1] = -sn
        rot[t, 64:128, 1] = sn
    return tab, cmat, csml, rot


def prep(x, norm1_g, w_in, ret_gn_g, lru_conv_w, lru_conv_b, lru_wa, lru_ba, lru_wx, lru_bx,
         lru_lambda, lru_norm_g, w_out, norm2_g, ffn_w_gate, ffn_w_up, ffn_w_down, final_g):
    f32 = np.float32
    x = np.asarray(x, f32)
    w_in_t = np.stack([_tile_w(np.asarray(w_in[l], f32), KC) for l in range(DEPTH)])
    w_out_t = np.stack([_tile_w(np.asarray(w_out[l], f32), KC) for l in range(DEPTH)])
    w_gate_t = np.stack([_tile_w(np.asarray(ffn_w_gate[l], f32), KC) for l in range(DEPTH)])
    w_up_t = np.stack([_tile_w(np.asarray(ffn_w_up[l], f32), KC) for l in range(DEPTH)])
    wd = np.asarray(ffn_w_down, f32)
    w_down_t = np.stack([np.stack([_tile_w(wd[l, hf * FH * 128:(hf + 1) * FH * 128], FH) for hf in range(FQ)])
                         for l in range(DEPTH)])
    wgates = np.zeros((DEPTH, 128, 16, 128), f32)
    for l in range(DEPTH):
        for gi, wsrc in enumerate((lru_wa, lru_wx)):
            ws = np.asarray(wsrc[l], f32)
            for c in range(8):
                wgates[l, 0:64, gi * 8 + c, 0:64] = ws[2 * c]
                wgates[l, 64:128, gi * 8 + c, 64:128] = ws[2 * c + 1]
    pvec = np.zeros((128, PV_TOT), f32)

    def colmaj(v, n):
        return np.asarray(v, f32).reshape(n, 128).T

    for l in range(DEPTH):
        pb = PV_L * l
        pvec[:, pb + PV_N1G:pb + PV_N1G + 16] = colmaj(norm1_g[l], 16)
        pvec[:, pb + PV_N2G:pb + PV_N2G + 16] = colmaj(norm2_g[l], 16)
        pvec[:, pb + PV_GNG:pb + PV_GNG + 8] = colmaj(ret_gn_g[l], 8)
        for j in range(4):
            pvec[:, pb + PV_CW + j * 8:pb + PV_CW + (j + 1) * 8] = colmaj(lru_conv_w[l][j], 8)
        pvec[:, pb + PV_CB:pb + PV_CB + 8] = colmaj(lru_conv_b[l], 8)
        pvec[:, pb + PV_BA:pb + PV_BA + 8] = colmaj(lru_ba[l], 8)
        pvec[:, pb + PV_BX:pb + PV_BX + 8] = colmaj(lru_bx[l], 8)
        pvec[:, pb + PV_LAM:pb + PV_LAM + 8] = colmaj(lru_lambda[l], 8)
        pvec[:, pb + PV_LNG:pb + PV_LNG + 8] = colmaj(lru_norm_g[l], 8)
    pvec[:, PV_FINAL:PV_FINAL + 16] = colmaj(final_g, 16)
    tab, cmat, csml, rot = _host_tables()

    def x_tiles(b):
        xb_ = x[b]
        return np.ascontiguousarray(xb_.reshape(NT, T, KC, 128).transpose(0, 3, 2, 1))

    shared = dict(w_in=w_in_t, w_out=w_out_t, w_gate=w_gate_t, w_up=w_up_t, w_down=w_down_t,
                  w_gates=wgates, pvec=pvec, rot=rot, tab=tab, cmat=cmat, csml=csml)
    return shared, x_tiles


def kernel(**inputs):
    f32 = np.float32
    shared, x_tiles = prep(**inputs)
    B = np.asarray(inputs["x"]).shape[0]
    busy = [0, 1, 4, 5]
    zeros = {k: np.zeros_like(v) for k, v in shared.items()}
    zx = np.zeros((NT, 128, KC, T), f32)
    in_maps = []
    for c in range(NCORES):
        if c in busy:
            m = dict(shared)
            m["x_in"] = x_tiles(busy.index(c))
        else:
            m = dict(zeros)
            m["x_in"] = zx
        in_maps.append(m)
    nc = build_program()
    res = run_bass_kernel_spmd(nc, in_maps, core_ids=list(range(NCORES)))
    out = np.zeros((B, NT * T, D), f32)
    for b in range(B):
        yt = np.asarray(res.results[busy[b]]["y_out"], f32)
        out[b] = yt.transpose(0, 3, 2, 1).reshape(NT * T, D)
    return out
```
